# Optimizing a Trainium2 kernel written in Bass

```python
import math
import jax
import jax.numpy as jnp
from jax import lax
import numpy as np

D_MODEL = 1024
BATCH = 32
SEQ = 2048
DEPTH = 1

D_FF = 2816
DN_HEADS = 8
DN_HEAD_DIM = 64
DN_WIDTH = DN_HEADS * DN_HEAD_DIM
CONV_WIDTH = 4
CHUNK = 64
S5_GROUP_CH = 16
S5_GROUPS = 32
S5_WIDTH = S5_GROUPS * S5_GROUP_CH
S5_STATE = 64
N_MOD = 9
EPS = 1e-6
IN_WIDTH = 4 * DN_WIDTH + 2 * DN_HEADS + S5_WIDTH + 2 * D_MODEL

kernel_name = 'hybrid_deltanet_s5_macaron'


def rmsnorm(x, gain):
    x32 = x.astype(jnp.float32)
    y = x32 * lax.rsqrt(jnp.mean(x32 * x32, axis=-1, keepdims=True) + EPS)
    return (y * gain.astype(jnp.float32)).astype(x.dtype)


def modulate(x, shift, scale):
    return x * (1 + scale) + shift


def swiglu(x, w1, w3, w2):
    return (jax.nn.silu(x @ w1) * (x @ w3)) @ w2


def l2norm(t):
    return t * lax.rsqrt(jnp.sum(t * t, axis=-1, keepdims=True) + EPS)


def causal_depthwise_conv(x, w):
    return lax.conv_general_dilated(
        x, w[:, None, :].astype(x.dtype), window_strides=(1,),
        padding=[(CONV_WIDTH - 1, 0)], dimension_numbers=('NWC', 'WIO', 'NWC'),
        feature_group_count=x.shape[-1])


def split_combined(p):
    sizes = (DN_WIDTH, DN_WIDTH, DN_WIDTH, DN_WIDTH, DN_HEADS, DN_HEADS, S5_WIDTH, D_MODEL, D_MODEL)
    parts = []
    start = 0
    for size in sizes:
        parts.append(p[..., start:start + size])
        start += size
    return parts


def gated_deltanet(q, k, v, z, beta_logit, decay_logit, conv_w, a_log, dt_bias, g_onorm):
    f32 = jnp.float32
    dtype = q.dtype
    bsz, seq, _ = q.shape
    n_chunks = seq // CHUNK
    qkv = jax.nn.silu(causal_depthwise_conv(jnp.concatenate([q, k, v], axis=-1), conv_w)).astype(f32)
    q, k, v = jnp.split(qkv, 3, axis=-1)

    def to_chunks(t):
        return t.reshape(bsz, n_chunks, CHUNK, DN_HEADS, DN_HEAD_DIM).transpose(0, 3, 1, 2, 4)

    def to_chunks_h(t):
        return t.reshape(bsz, n_chunks, CHUNK, DN_HEADS).transpose(0, 3, 1, 2)

    q = l2norm(to_chunks(q)) * (DN_HEAD_DIM ** -0.5)
    k = l2norm(to_chunks(k))
    v = to_chunks(v)
    beta = to_chunks_h(jax.nn.sigmoid(beta_logit.astype(f32)))
    log_alpha = -jnp.exp(a_log.astype(f32)) * jax.nn.softplus(decay_logit.astype(f32) + dt_bias.astype(f32))
    g_cum = jnp.cumsum(to_chunks_h(log_alpha), axis=-1)
    causal = jnp.tril(jnp.ones((CHUNK, CHUNK), dtype=bool))
    strict = jnp.tril(jnp.ones((CHUNK, CHUNK), dtype=bool), k=-1)
    decay = jnp.exp(jnp.where(causal, g_cum[..., :, None] - g_cum[..., None, :], -jnp.inf))
    k_beta = k * beta[..., None]
    kk = jnp.where(strict, jnp.einsum('bhnik,bhnjk->bhnij', k_beta, k) * decay, 0.0)
    lhs = kk + jnp.eye(CHUNK, dtype=f32)
    rhs = jnp.concatenate([v * beta[..., None], k_beta * jnp.exp(g_cum)[..., None]], axis=-1)
    sol = lax.linalg.triangular_solve(lhs, rhs, left_side=True, lower=True, unit_diagonal=True)
    u_c, w_c = sol[..., :DN_HEAD_DIM], sol[..., DN_HEAD_DIM:]
    attn = jnp.einsum('bhnik,bhnjk->bhnij', q, k) * decay
    q_dec = q * jnp.exp(g_cum)[..., None]
    g_last = g_cum[..., -1]
    k_dec = k * jnp.exp(g_last[..., None] - g_cum)[..., None]

    def step(state, xs):
        q_i, k_i, u_i, w_i, a_i, gl_i = xs
        v_new = u_i - jnp.einsum('bhck,bhkv->bhcv', w_i, state)
        o_i = jnp.einsum('bhck,bhkv->bhcv', q_i, state) + jnp.einsum('bhij,bhjv->bhiv', a_i, v_new)
        state = state * jnp.exp(gl_i)[..., None, None] + jnp.einsum('bhck,bhcv->bhkv', k_i, v_new)
        return state, o_i

    xs = (q_dec, k_dec, u_c, w_c, attn, g_last)
    xs = tuple(jnp.moveaxis(t, 2, 0) for t in xs)
    state0 = jnp.zeros((bsz, DN_HEADS, DN_HEAD_DIM, DN_HEAD_DIM), f32)
    _, o = lax.scan(step, state0, xs)
    o = o.transpose(1, 0, 3, 2, 4).reshape(bsz, seq, DN_HEADS, DN_HEAD_DIM)
    gate = jax.nn.silu(z.astype(f32)).reshape(bsz, seq, DN_HEADS, DN_HEAD_DIM)
    o = o * lax.rsqrt(jnp.mean(o * o, axis=-1, keepdims=True) + EPS) * g_onorm.astype(f32) * gate
    return o.reshape(bsz, seq, DN_WIDTH).astype(dtype)


def s5_ssm(u_in, lam_re, lam_im, log_step, b_re, b_im, c_re, c_im, d_skip, w_glu, b_glu):
    f32 = jnp.float32
    dtype = u_in.dtype
    bsz, seq, _ = u_in.shape
    u = u_in.astype(f32).reshape(bsz, seq, S5_GROUPS, S5_GROUP_CH)
    lam_re = jnp.minimum(lam_re.astype(f32), -1e-4)
    lam_im = lam_im.astype(f32)
    step = jnp.exp(log_step.astype(f32))[:, None]
    mag = jnp.exp(lam_re * step)
    ang = lam_im * step
    lb_re = mag * jnp.cos(ang)
    lb_im = mag * jnp.sin(ang)
    den = lam_re * lam_re + lam_im * lam_im
    coef_re = ((lb_re - 1.0) * lam_re + lb_im * lam_im) / den
    coef_im = (lb_im * lam_re - (lb_re - 1.0) * lam_im) / den
    b_re = b_re.astype(f32)
    b_im = b_im.astype(f32)
    bb_re = coef_re[..., None] * b_re - coef_im[..., None] * b_im
    bb_im = coef_re[..., None] * b_im + coef_im[..., None] * b_re
    bu_re = jnp.einsum('bsgc,gpc->bsgp', u, bb_re)
    bu_im = jnp.einsum('bsgc,gpc->bsgp', u, bb_im)
    a_re = jnp.broadcast_to(lb_re, (1, seq, S5_GROUPS, S5_STATE))
    a_im = jnp.broadcast_to(lb_im, (1, seq, S5_GROUPS, S5_STATE))

    def combine(e1, e2):
        a1r, a1i, b1r, b1i = e1
        a2r, a2i, b2r, b2i = e2
        return (a2r * a1r - a2i * a1i,
                a2r * a1i + a2i * a1r,
                a2r * b1r - a2i * b1i + b2r,
                a2r * b1i + a2i * b1r + b2i)

    _, _, x_re, x_im = lax.associative_scan(combine, (a_re, a_im, bu_re, bu_im), axis=1)
    y = (jnp.einsum('bsgp,gcp->bsgc', x_re, c_re.astype(f32))
         - jnp.einsum('bsgp,gcp->bsgc', x_im, c_im.astype(f32))
         + d_skip.astype(f32).reshape(S5_GROUPS, S5_GROUP_CH) * u)
    y = jax.nn.gelu(y.reshape(bsz, seq, S5_WIDTH))
    y = y * jax.nn.sigmoid(y @ w_glu.astype(f32) + b_glu.astype(f32))
    return y.astype(dtype)


def hybrid_layer(h, c, w_ada, b_ada, g_ffn1, w1_ffn1, w3_ffn1, w2_ffn1, g_mix, w_in, conv_qkv,
                 a_log, dt_bias, g_onorm, lam_re, lam_im, log_step, b_re, b_im, c_re, c_im,
                 d_skip, w_glu, b_glu, w_proj_a, w_proj_b, w_out, g_ffn2, w1_ffn2, w3_ffn2, w2_ffn2):
    mod = jax.nn.silu(c) @ w_ada + b_ada
    sh1, sc1, gt1, sh2, sc2, gt2, sh3, sc3, gt3 = [m[:, None, :] for m in jnp.split(mod, N_MOD, axis=-1)]
    h = h + 0.5 * gt1 * swiglu(modulate(rmsnorm(h, g_ffn1), sh1, sc1), w1_ffn1, w3_ffn1, w2_ffn1)
    u = modulate(rmsnorm(h, g_mix), sh2, sc2)
    q, k, v, z, beta_logit, decay_logit, s5_in, gate_a, gate_b = split_combined(u @ w_in)
    y_a = gated_deltanet(q, k, v, z, beta_logit, decay_logit, conv_qkv, a_log, dt_bias, g_onorm) @ w_proj_a
    y_b = s5_ssm(s5_in, lam_re, lam_im, log_step, b_re, b_im, c_re, c_im, d_skip, w_glu, b_glu) @ w_proj_b
    merged = jax.nn.sigmoid(gate_a) * y_a + jax.nn.sigmoid(gate_b) * y_b
    h = h + gt2 * (merged @ w_out)
    h = h + 0.5 * gt3 * swiglu(modulate(rmsnorm(h, g_ffn2), sh3, sc3), w1_ffn2, w3_ffn2, w2_ffn2)
    return h


def setup_inputs(seed: int = 0) -> dict:
    key = jax.random.key(seed)
    ks = jax.random.split(key, 40)
    f32 = jnp.float32
    L = DEPTH

    def nrm(k, shape, scale):
        return jax.random.normal(k, shape, f32) * scale

    def log_uniform(k, shape, lo, hi):
        return jax.random.uniform(k, shape, f32, math.log(lo), math.log(hi))

    dt = jnp.exp(log_uniform(ks[12], (L, DN_HEADS), 1e-3, 1e-1))
    n_idx = jnp.arange(S5_STATE, dtype=f32)
    return {
        'x': nrm(ks[0], (BATCH, SEQ, D_MODEL), 1.0),
        'c': nrm(ks[1], (BATCH, D_MODEL), 1.0),
        'w_ada': nrm(ks[2], (L, D_MODEL, N_MOD * D_MODEL), 0.5 * D_MODEL ** -0.5),
        'b_ada': nrm(ks[3], (L, N_MOD * D_MODEL), 0.02),
        'g_ffn1': 1.0 + nrm(ks[4], (L, D_MODEL), 0.02),
        'w1_ffn1': nrm(ks[5], (L, D_MODEL, D_FF), D_MODEL ** -0.5),
        'w3_ffn1': nrm(ks[6], (L, D_MODEL, D_FF), D_MODEL ** -0.5),
        'w2_ffn1': nrm(ks[7], (L, D_FF, D_MODEL), D_FF ** -0.5),
        'g_mix': 1.0 + nrm(ks[8], (L, D_MODEL), 0.02),
        'w_in': nrm(ks[9], (L, D_MODEL, IN_WIDTH), D_MODEL ** -0.5),
        'conv_qkv': nrm(ks[10], (L, CONV_WIDTH, 3 * DN_WIDTH), CONV_WIDTH ** -0.5),
        'a_log': jnp.log(jax.random.uniform(ks[11], (L, DN_HEADS), f32, 1.0, 16.0)),
        'dt_bias': dt + jnp.log(-jnp.expm1(-dt)),
        'g_onorm': 1.0 + nrm(ks[13], (L, DN_HEAD_DIM), 0.02),
        'lam_re': -0.5 + nrm(ks[14], (L, S5_GROUPS, S5_STATE), 0.01),
        'lam_im': math.pi * n_idx + nrm(ks[15], (L, S5_GROUPS, S5_STATE), 0.01),
        'log_step': log_uniform(ks[16], (L, S5_GROUPS), 1e-3, 1e-1),
        'b_re': nrm(ks[17], (L, S5_GROUPS, S5_STATE, S5_GROUP_CH), (2 * S5_GROUP_CH) ** -0.5),
        'b_im': nrm(ks[18], (L, S5_GROUPS, S5_STATE, S5_GROUP_CH), (2 * S5_GROUP_CH) ** -0.5),
        'c_re': nrm(ks[19], (L, S5_GROUPS, S5_GROUP_CH, S5_STATE), S5_STATE ** -0.5),
        'c_im': nrm(ks[20], (L, S5_GROUPS, S5_GROUP_CH, S5_STATE), S5_STATE ** -0.5),
        'd_skip': nrm(ks[21], (L, S5_WIDTH), 1.0),
        'w_glu': nrm(ks[22], (L, S5_WIDTH, S5_WIDTH), S5_WIDTH ** -0.5),
        'b_glu': nrm(ks[23], (L, S5_WIDTH), 0.02),
        'w_proj_a': nrm(ks[24], (L, DN_WIDTH, D_MODEL), DN_WIDTH ** -0.5),
        'w_proj_b': nrm(ks[25], (L, S5_WIDTH, D_MODEL), S5_WIDTH ** -0.5),
        'w_out': nrm(ks[26], (L, D_MODEL, D_MODEL), D_MODEL ** -0.5),
        'g_ffn2': 1.0 + nrm(ks[27], (L, D_MODEL), 0.02),
        'w1_ffn2': nrm(ks[28], (L, D_MODEL, D_FF), D_MODEL ** -0.5),
        'w3_ffn2': nrm(ks[29], (L, D_MODEL, D_FF), D_MODEL ** -0.5),
        'w2_ffn2': nrm(ks[30], (L, D_FF, D_MODEL), D_FF ** -0.5),
        'g_final': 1.0 + nrm(ks[31], (D_MODEL,), 0.02),
    }


def reference(x, c, w_ada, b_ada, g_ffn1, w1_ffn1, w3_ffn1, w2_ffn1, g_mix, w_in, conv_qkv,
              a_log, dt_bias, g_onorm, lam_re, lam_im, log_step, b_re, b_im, c_re, c_im,
              d_skip, w_glu, b_glu, w_proj_a, w_proj_b, w_out, g_ffn2, w1_ffn2, w3_ffn2, w2_ffn2,
              g_final):
    h = x
    for layer in range(DEPTH):
        h = hybrid_layer(
            h, c, w_ada[layer], b_ada[layer], g_ffn1[layer], w1_ffn1[layer], w3_ffn1[layer],
            w2_ffn1[layer], g_mix[layer], w_in[layer], conv_qkv[layer], a_log[layer],
            dt_bias[layer], g_onorm[layer], lam_re[layer], lam_im[layer], log_step[layer],
            b_re[layer], b_im[layer], c_re[layer], c_im[layer], d_skip[layer], w_glu[layer],
            b_glu[layer], w_proj_a[layer], w_proj_b[layer], w_out[layer], g_ffn2[layer],
            w1_ffn2[layer], w3_ffn2[layer], w2_ffn2[layer])
    return rmsnorm(h, g_final)
```

```python
import os
import numpy as np
from contextlib import ExitStack
import concourse.bass as bass
import concourse.mybir as mybir
from concourse.bass_utils import run_bass_kernel_spmd

F32 = mybir.dt.float32
BF16 = mybir.dt.bfloat16
AF = mybir.ActivationFunctionType
ALU = mybir.AluOpType
AX = mybir.AxisListType

D = 1024
SEQ = 2048
DFF = 2816
NCORE = 8
EPS = 1e-6
NSLOT = 26
SLOT = 1024
WBN = 4
WBEL = 3072


class Sched:
    COMPUTE = ("pe", "act", "dve", "pool")

    def __init__(self, nc, es, n_dma_sems=24):
        self.nc = nc
        self.sem = {}
        self.cnt = {}
        for e in self.COMPUTE:
            self.sem[e] = es.enter_context(nc.semaphore("s_" + e))
            self.cnt[e] = 0
        self.dsem = [es.enter_context(nc.semaphore("d%d" % i)) for i in range(n_dma_sems)]
        self.dcnt = [0] * n_dma_sems
        self.dnext = 0
        self.dq = {}
        self.seen = {e: {} for e in ("pe", "act", "dve", "pool", "sp")}
        self.last_w = {}
        self.readers = {}
        self.prog = {e: [] for e in ("pe", "act", "dve", "pool", "sp")}
        self.ninst = 0

    def _deps(self, eng, reads, writes):
        deps = {}

        def add(tok):
            s, v = tok
            if deps.get(s, 0) < v:
                deps[s] = v
        for k in reads:
            if k in self.last_w:
                add(self.last_w[k])
        for k in writes:
            if k in self.last_w:
                add(self.last_w[k])
            for t in self.readers.get(k, ()):
                add(t)
        out = []
        for s, v in deps.items():
            if eng == "pe" and s is self.sem["pe"]:
                continue
            if self.seen[eng].get(s, 0) >= v:
                continue
            self.seen[eng][s] = v
            out.append((s, v))
        return out

    def _record(self, tok, reads, writes):
        for k in reads:
            self.readers.setdefault(k, []).append(tok)
        for k in writes:
            self.last_w[k] = tok
            self.readers[k] = []

    def op(self, eng, fn, reads=(), writes=()):
        waits = self._deps(eng, reads, writes)
        self.cnt[eng] += 1
        tok = (self.sem[eng], self.cnt[eng])
        sem = self.sem[eng]

        def emit(e, fn=fn, waits=waits, sem=sem):
            for s, v in waits:
                e.wait_ge(s, v)
            fn(e).then_inc(sem, 1)
        self.prog[eng].append(emit)
        self._record(tok, reads, writes)
        self.ninst += 1

    def dma(self, q, fn, reads=(), writes=()):
        n = len(self.dsem)
        lo, hi = (0, (2 * n) // 3) if q == "sp" else ((2 * n) // 3, n)
        cur = self.dq.get(q, lo)
        i = cur
        self.dq[q] = lo + ((cur + 1 - lo) % (hi - lo))
        s = self.dsem[i]
        waits = self._deps(q, reads, writes)
        if self.dcnt[i] > 0 and self.seen[q].get(s, 0) < self.dcnt[i]:
            waits.append((s, self.dcnt[i]))
            self.seen[q][s] = self.dcnt[i]
        self.dcnt[i] += 16
        tok = (s, self.dcnt[i])

        def emit(e, fn=fn, waits=waits, s=s):
            for ss, v in waits:
                e.wait_ge(ss, v)
            fn(e).then_inc(s, 16)
        self.prog[q].append(emit)
        self._record(tok, reads, writes)
        self.ninst += 1
        return tok

    def final_wait(self, eng, keys):
        waits = self._deps(eng, keys, ())

        def emit(e, waits=waits):
            for s, v in waits:
                e.wait_ge(s, v)
        self.prog[eng].append(emit)

    def run(self, block):
        P = self.prog

        @block.tensor
        def _(e):
            for f in P["pe"]:
                f(e)

        @block.scalar
        def _(e):
            for f in P["act"]:
                f(e)

        @block.vector
        def _(e):
            for f in P["dve"]:
                f(e)

        @block.gpsimd
        def _(e):
            for f in P["pool"]:
                f(e)

        @block.sync
        def _(e):
            for f in P["sp"]:
                f(e)


V_G1, V_GM, V_G2, V_GF = 0, 8, 16, 24
V_BGLU, V_DSK = 32, 36
V_CQK = 40
V_CV = 72
V_LRE, V_LIM, V_LST = 88, 104, 120
V_ALOG, V_DTB, V_GON = 136, 144, 152
NV = 216
C_ID, C_TRI, C_NEG, C_STR, C_ONE, C_IP32 = 0, 128, 256, 384, 512, 640
NCON = 672


def build(nseq=4, stage=99):
    nc = bass.Bass("TRN2", target_bir_lowering=False)

    def din(name, shape):
        return nc.dram_tensor(name, list(shape), F32, kind="ExternalInput").ap()

    xT = din("xT", [nseq, 128, 8, SEQ])
    cT = din("cT", [128, 8, 4])
    wada = din("wada", [72, 128, 8 * 128])
    bada = din("bada", [128, 72])
    w1b_d = din("w1b", [22, 128, 8 * 128])
    w3b_d = din("w3b", [22, 128, 8 * 128])
    w2b_d = din("w2b", [8, 128, 22 * 128])
    w1 = [din("w1a", [22, 128, 8 * 128]), None]
    w3 = [din("w3a", [22, 128, 8 * 128]), None]
    w2 = [din("w2a", [8, 128, 22 * 128]), None]
    wq = din("wq", [8, 128, 8 * 64])
    wk = din("wk", [8, 128, 8 * 64])
    wv = din("wv", [4, 128, 8 * 128])
    wz = din("wz", [2, 128, 8 * 256])
    wba = din("wba", [1, 128, 8 * 16])
    ws5 = din("ws5", [4, 128, 8 * 128])
    wga = din("wga", [8, 128, 8 * 128])
    wgb = din("wgb", [8, 128, 8 * 128])
    wpa = din("wpa", [8, 128, 4 * 128])
    wpb = din("wpb", [8, 128, 4 * 128])
    wglu = din("wglu", [4, 128, 4 * 128])
    wout = din("wout", [8, 128, 8 * 128])
    vecs_d = din("vecs", [128, NV])
    cons_d = din("cons", [128, NCON])
    s5bc_d = din("s5bc", [128, 4 * 256])
    masks_d = din("masks", [128, 9 * 128])
    oT = nc.dram_tensor("oT", [nseq, 128, 8, SEQ], F32, kind="ExternalOutput").ap()
    tab_d = nc.dram_tensor("tabs", [16, 2, 128, SEQ], BF16, kind="Internal").ap()

    with ExitStack() as es:
        S = Sched(nc, es)

        def sb(name, shape, dt=F32):
            return es.enter_context(nc.sbuf_tensor(name, list(shape), dt))

        hT = sb("hT", [128, 8, 1024])
        arena = sb("arena", [128, NSLOT * SLOT])
        wsm = [sb("wsmall%d" % i, [128, 1024], BF16) for i in range(7)]
        wlg = [sb("wlarge%d" % i, [128, 2816], BF16) for i in range(3)]
        vecs = sb("vecs_s", [128, NV])
        cons = sb("cons_s", [128, NCON])
        consb = sb("consb", [128, 3 * 128], BF16)
        convqk = sb("convqk", [128, 8, 4, 64], BF16)
        convv = sb("convv", [128, 4, 4, 128], BF16)
        BbT = sb("BbT", [128, 4, 2, 128], BF16)
        CT = sb("CT", [128, 16, 2, 32], BF16)
        Dd = sb("Dd", [128, 4, 32], BF16)
        s5v = sb("s5v", [128, 256])
        modT = sb("modT", [128, 72, 4])
        GS = sb("GS", [128, 3, 8, 4])
        HG = sb("HG", [128, 3, 8, 4])
        sm = sb("sm", [128, 512])
        Sst = sb("Sst", [64, 512])
        Sbf = sb("Sbf", [64, 512], BF16)
        zst = sb("zst", [128, 16, 2])
        qkpre = sb("qkpre", [128, 8, 515], BF16)
        vpre = sb("vpre", [128, 4, 515], BF16)
        vnb = sb("vnb", [128, 512], BF16)
        masks = sb("masks_s", [128, 9 * 128], BF16)
        s5i = sb("s5i", [128, 16], mybir.dt.int32)
        psum = es.enter_context(nc.psum_tensor("ps", [128, 8 * 512], F32))
        block = es.enter_context(nc.Block())

        def region(s0, off_f32, n_f32, dt=F32, parts=128):
            a = s0 * SLOT + off_f32
            ap = arena[0:parts, a:a + n_f32]
            if dt == BF16:
                ap = ap.bitcast(BF16)
            keys = [("sq", i) for i in range(a // 256, (a + n_f32 - 1) // 256 + 1)]
            return ap, keys

        psn = [0]

        def getps():
            i = psn[0] % 8
            psn[0] += 1
            return psum[:, i * 512:(i + 1) * 512], [("ps", i)]

        ps2n = [0]

        def getps2():
            i = (ps2n[0] % 4) * 2
            ps2n[0] += 1
            return psum[:, i * 512:(i + 2) * 512], [("ps", i), ("ps", i + 1)]

        wsn = [0]
        wln = [0]

        def load_w(W, m, nel):
            ap, kcv = W
            if nel <= 1024:
                i = wsn[0] % len(wsm)
                wsn[0] += 1
                buf, key = wsm[i], ("ws", i)
            else:
                i = wln[0] % len(wlg)
                wln[0] += 1
                buf, key = wlg[i], ("wl", i)
            dst = buf[:, 0:nel]
            S.dma("sp", lambda e, dst=dst, ap=ap, m=m: e.dma_start(out=dst, in_=ap[m]), reads=kcv, writes=[key])
            return buf, [key]

        def conv_w(name, src):
            shp = list(src.shape)
            dst = nc.dram_tensor(name + "_bf", shp, BF16, kind="Internal").ap()
            S.dma("pool", lambda e: e.dma_start(out=dst.rearrange("m p n -> (m p) n"), in_=src.rearrange("m p n -> (m p) n"),
                                               max_dma_last_dim=4096), writes=[("wcv", name)])
            return dst, [("wcv", name)]

        def mm(out, lhsT, rhs, reads, writes, start=True, stop=True, tp=None):
            if tp is None:
                S.op("pe", lambda e: e.matmul(out, lhsT=lhsT, rhs=rhs, start=start, stop=stop), reads=reads, writes=writes)
            else:
                S.op("pe", lambda e: e.matmul(out, lhsT=lhsT, rhs=rhs, start=start, stop=stop, tile_position=tp),
                     reads=reads, writes=writes)

        def act(out, in_, func, reads, writes, bias=None, scale=None):
            kw = {}
            if bias is not None:
                kw["bias"] = bias
            if scale is not None:
                kw["scale"] = scale
            S.op("act", lambda e: e.activation(out=out, in_=in_, func=func, **kw), reads=reads, writes=writes)

        def tt(eng, out, in0, in1, op, reads, writes):
            S.op(eng, lambda e: e.tensor_tensor(out=out, in0=in0, in1=in1, op=op), reads=reads, writes=writes)

        def ts(eng, out, in0, s1, op0, reads, writes, s2=None, op1=None):
            if op1 is None:
                S.op(eng, lambda e: e.tensor_scalar(out=out, in0=in0, scalar1=s1, scalar2=None, op0=op0), reads=reads, writes=writes)
            else:
                S.op(eng, lambda e: e.tensor_scalar(out=out, in0=in0, scalar1=s1, scalar2=s2, op0=op0, op1=op1),
                     reads=reads, writes=writes)

        def stt(eng, out, in0, scalar, in1, op0, op1, reads, writes):
            S.op(eng, lambda e: e.scalar_tensor_tensor(out=out, in0=in0, scalar=scalar, in1=in1, op0=op0, op1=op1),
                 reads=reads, writes=writes)

        def cpy(eng, out, in_, reads, writes):
            if eng == "act":
                S.op("act", lambda e: e.copy(out=out, in_=in_), reads=reads, writes=writes)
            else:
                S.op(eng, lambda e: e.tensor_copy(out=out, in_=in_), reads=reads, writes=writes)

        ident = cons[:, C_ID:C_ID + 128]
        tri = cons[:, C_TRI:C_TRI + 128]
        negm = cons[:, C_NEG:C_NEG + 128]
        strict = cons[:, C_STR:C_STR + 128]
        onesf = cons[:, C_ONE:C_ONE + 128]
        ip32 = cons[:, C_IP32:C_IP32 + 32]
        identb = consb[:, 0:128]
        onesD = consb[:, 128:256]
        bones = consb[:, 256:384]
        KC = ["cons"]

        S.dma("sp", lambda e: e.dma_start(out=vecs[:], in_=vecs_d), writes=["vecs"])
        S.dma("sp", lambda e: e.dma_start(out=cons[:], in_=cons_d), writes=["cons"])
        S.dma("pool", lambda e: e.dma_start(out=masks[:], in_=masks_d), writes=["masks"])
        w1 = [conv_w("w1a", w1[0]), None]
        w3 = [conv_w("w3a", w3[0]), None]
        w2 = [conv_w("w2a", w2[0]), None]
        wq = conv_w("wq", wq)
        wk = conv_w("wk", wk)
        wv = conv_w("wv", wv)
        wz = conv_w("wz", wz)
        wba = conv_w("wba", wba)
        ws5 = conv_w("ws5", ws5)
        wglu = conv_w("wglu", wglu)
        wpa = conv_w("wpa", wpa)
        wpb = conv_w("wpb", wpb)
        wga = conv_w("wga", wga)
        wgb = conv_w("wgb", wgb)
        wout = conv_w("wout", wout)
        w1[1] = conv_w("w1b", w1b_d)
        w3[1] = conv_w("w3b", w3b_d)
        w2[1] = conv_w("w2b", w2b_d)
        epsc = s5v[:, 180:181]
        eps64 = s5v[:, 181:182]
        onec = s5v[:, 182:183]
        c64 = s5v[:, 183:184]
        S.op("dve", lambda e: e.memset(epsc, EPS), writes=["s5vc"])
        S.op("dve", lambda e: e.memset(eps64, 64.0 * EPS), writes=["s5vc"])
        S.op("dve", lambda e: e.memset(onec, 1.0), writes=["s5vc"])
        S.op("dve", lambda e: e.memset(c64, 64.0), writes=["s5vc"])
        cpy("dve", identb, ident, ["cons"], ["consb"])
        ts("dve", onesD, onesf, 1.0 / D, ALU.mult, ["cons"], ["consb"])
        S.op("dve", lambda e: e.memset(bones, 0.0), writes=["consb"])
        S.op("dve", lambda e: e.memset(consb[0:64, 256:320], 1.0), writes=["consb"])
        S.op("dve", lambda e: e.memset(consb[64:128, 320:384], 1.0), writes=["consb"])
        for h in range(8):
            for j in range(4):
                col = V_CQK + h * 4 + j
                ts("dve", convqk[0:64, h, j, :], ident[0:64, 0:64], vecs[0:64, col:col + 1], ALU.mult,
                   ["cons", "vecs"], ["convqk"])
                ts("dve", convqk[64:128, h, j, :], ident[64:128, 64:128], vecs[64:128, col:col + 1], ALU.mult,
                   ["cons", "vecs"], ["convqk"])
        for m in range(4):
            for j in range(4):
                col = V_CV + m * 4 + j
                ts("dve", convv[:, m, j, :], ident, vecs[:, col:col + 1], ALU.mult, ["cons", "vecs"], ["convv"])
        negexpA = s5v[:, 184:192]
        act(negexpA, vecs[:, V_ALOG:V_ALOG + 8], AF.Exp, ["vecs"], ["negexpA"])
        ts("dve", negexpA, negexpA, -1.0, ALU.mult, ["negexpA"], ["negexpA"])

        csil, kcs = region(0, 0, 32)
        S.dma("sp", lambda e: e.dma_start(out=csil, in_=cT.rearrange("p k b -> p (k b)")), writes=kcs)
        act(csil, csil, AF.Silu, kcs, kcs)
        badas, kba = region(0, 64, 72)
        S.dma("sp", lambda e: e.dma_start(out=badas, in_=bada), writes=kba)
        psm, kpm = getps()
        for m in range(72):
            wt, kw = region(1 + (m % 8), 0, 1024)
            S.dma("sp", lambda e, wt=wt, m=m: e.dma_start(out=wt, in_=wada[m]), writes=kw)
            for k in range(8):
                mm(psm[:, m * 4:m * 4 + 4], wt[:, k * 128:(k + 1) * 128], csil[:, k * 4:k * 4 + 4],
                   kw + kcs, kpm, start=(k == 0), stop=(k == 7))
        mod3 = modT[:]
        tt("dve", mod3, psm[:, 0:288].rearrange("p (m b) -> p m b", b=4),
           badas.unsqueeze(2).broadcast_to([128, 72, 4]), ALU.add, kpm + kba, ["modT"])
        for i, (gcol, hgs) in enumerate(((V_G1, 0.5), (V_GM, 1.0), (V_G2, 0.5))):
            sc = modT[:, (3 * i + 1) * 8:(3 * i + 2) * 8, :]
            gt = modT[:, (3 * i + 2) * 8:(3 * i + 3) * 8, :]
            ts("dve", GS[:, i], sc, 1.0, ALU.add, ["modT"], ["GS"])
            tt("dve", GS[:, i], GS[:, i], vecs[:, gcol:gcol + 8].unsqueeze(2).broadcast_to([128, 8, 4]), ALU.mult,
               ["GS", "vecs"], ["GS"])
            ts("dve", HG[:, i], gt, hgs, ALU.mult, ["modT"], ["HG"])

        s5bc, ks5 = region(10, 0, 1024)
        S.dma("sp", lambda e: e.dma_start(out=s5bc, in_=s5bc_d), writes=ks5)
        lre = vecs[:, V_LRE:V_LRE + 16]
        lim = vecs[:, V_LIM:V_LIM + 16]

        def sv(i):
            return s5v[:, i * 16:(i + 1) * 16]
        stp, lrc, mag, ang, lbr, lbi, den, cre_, cim_, t1, t2 = [sv(i) for i in range(11)]
        K5 = ["s5v"]
        act(stp, vecs[:, V_LST:V_LST + 16], AF.Exp, ["vecs"], K5)
        ts("dve", lrc, lre, -1e-4, ALU.min, ["vecs"], K5)
        tt("dve", t1, lrc, stp, ALU.mult, K5, K5)
        act(mag, t1, AF.Exp, K5, K5)
        tt("dve", ang, lim, stp, ALU.mult, K5 + ["vecs"], K5)
        PI = float(np.pi)
        C1 = 6.28125
        C2 = 2 * PI - C1
        for (dst, shift, tmp) in ((lbi, 0.0, t1), (lbr, 0.5 * PI, t2)):
            ts("dve", tmp, ang, shift, ALU.add, K5, K5)
            ts("dve", den, tmp, 1.0 / (2 * PI), ALU.mult, K5, K5)
            cpy("dve", s5i[:, 0:16], den, K5, ["s5i"])
            cpy("dve", den, s5i[:, 0:16], ["s5i"], K5)
            stt("dve", tmp, den, -C1, tmp, ALU.mult, ALU.add, K5, K5)
            stt("dve", tmp, den, -C2, tmp, ALU.mult, ALU.add, K5, K5)
            ts("dve", tmp, tmp, PI, ALU.min, K5, K5, s2=-PI, op1=ALU.max)
            act(dst, tmp, AF.Sin, K5, K5)
        tabre, ktr = region(20, 0, 2048)
        tabim, kti = region(22, 0, 2048)
        ttmp, ktt = region(24, 0, 1024)
        ttmp2, ktt2 = region(25, 0, 1024)
        seedr = s5v[:, 192:208]
        seedi = s5v[:, 208:224]
        cpy("dve", seedr, lbr, K5, ["seed"])
        cpy("dve", seedi, lbi, K5, ["seed"])
        def table_gen():
            for pt in range(16):
                if stage < 3:
                    break
                cpy("dve", tabre[:, 0:1], seedr[:, pt:pt + 1], ["seed"], ktr)
                cpy("dve", tabim[:, 0:1], seedi[:, pt:pt + 1], ["seed"], kti)
                mlen = 1
                while mlen < SEQ:
                    cr = tabre[:, mlen - 1:mlen]
                    ci = tabim[:, mlen - 1:mlen]
                    ts("dve", ttmp[:, 0:mlen], tabim[:, 0:mlen], ci, ALU.mult, kti, ktt)
                    ts("dve", ttmp2[:, 0:mlen], tabim[:, 0:mlen], cr, ALU.mult, kti + ktr, ktt2)
                    stt("dve", tabre[:, mlen:2 * mlen], tabre[:, 0:mlen], cr, ttmp[:, 0:mlen], ALU.mult, ALU.subtract,
                        ktr + ktt, ktr)
                    stt("dve", tabim[:, mlen:2 * mlen], tabre[:, 0:mlen], ci, ttmp2[:, 0:mlen], ALU.mult, ALU.add,
                        ktr + kti + ktt2, kti)
                    mlen *= 2
                S.dma("pool", lambda e, pt=pt: e.dma_start(out=tab_d[pt, 0], in_=tabre, max_dma_last_dim=4096), reads=ktr, writes=[("tab", pt)])
                S.dma("pool", lambda e, pt=pt: e.dma_start(out=tab_d[pt, 1], in_=tabim, max_dma_last_dim=4096), reads=kti, writes=[("tab", pt)])
                yield
        bg_tables = table_gen()
        tt("dve", lbr, lbr, mag, ALU.mult, K5, K5)
        tt("dve", lbi, lbi, mag, ALU.mult, K5, K5)
        tt("dve", den, lrc, lrc, ALU.mult, K5, K5)
        tt("dve", t1, lim, lim, ALU.mult, ["vecs"] + K5, K5)
        tt("dve", den, den, t1, ALU.add, K5, K5)
        S.op("dve", lambda e: e.reciprocal(out=den, in_=den), reads=K5, writes=K5)
        ts("dve", t2, lbr, -1.0, ALU.add, K5, K5)
        tt("dve", cre_, t2, lrc, ALU.mult, K5, K5)
        tt("dve", t1, lbi, lim, ALU.mult, K5 + ["vecs"], K5)
        tt("dve", cre_, cre_, t1, ALU.add, K5, K5)
        tt("dve", cre_, cre_, den, ALU.mult, K5, K5)
        tt("dve", cim_, lbi, lrc, ALU.mult, K5, K5)
        tt("dve", t1, t2, lim, ALU.mult, K5 + ["vecs"], K5)
        tt("dve", cim_, cim_, t1, ALU.subtract, K5, K5)
        tt("dve", cim_, cim_, den, ALU.mult, K5, K5)
        b_re = s5bc[:, 0:256].rearrange("p (t c) -> p t c", c=16)
        b_im = s5bc[:, 256:512].rearrange("p (t c) -> p t c", c=16)
        c_re = s5bc[:, 512:768].rearrange("p (t c) -> p t c", c=16)
        c_im = s5bc[:, 768:1024].rearrange("p (t c) -> p t c", c=16)
        bb, kbb = region(11, 0, 1024)
        bbr = bb[:, 0:256].rearrange("p (t c) -> p t c", c=16)
        bbi = bb[:, 256:512].rearrange("p (t c) -> p t c", c=16)
        btm = bb[:, 512:768].rearrange("p (t c) -> p t c", c=16)
        creb = cre_.unsqueeze(2).broadcast_to([128, 16, 16])
        cimb = cim_.unsqueeze(2).broadcast_to([128, 16, 16])
        tt("dve", bbr, b_re, creb, ALU.mult, ks5 + K5, kbb)
        tt("dve", btm, b_im, cimb, ALU.mult, ks5 + K5, kbb)
        tt("dve", bbr, bbr, btm, ALU.subtract, kbb, kbb)
        tt("dve", bbi, b_im, creb, ALU.mult, ks5 + K5, kbb)
        tt("dve", btm, b_re, cimb, ALU.mult, ks5 + K5, kbb)
        tt("dve", bbi, bbi, btm, ALU.add, kbb, kbb)
        blk, kbl = region(18, 0, 512, BF16)
        blkr = blk[:, 0:512].rearrange("p (t c) -> p t c", c=32)
        blki = blk[:, 512:1024].rearrange("p (t c) -> p t c", c=32)
        S.op("dve", lambda e: e.memset(blk, 0.0), writes=kbl)
        for (dst, src) in ((blkr, bbr), (blki, bbi)):
            cpy("dve", dst[0:64, :, 0:16], src[0:64], kbb, kbl)
            cpy("dve", dst[64:128, :, 16:32], src[64:128], kbb, kbl)
        for m in range(4):
            for ri, src in enumerate((blkr, blki)):
                pst, kps = getps()
                for q in range(4):
                    mm(pst[32 * q:32 * q + 32, 0:128], src[:, 4 * m + q, :], identb, kbl + ["consb"], kps, tp=(0, 32 * q))
                cpy("act", BbT[:, m, ri, :], pst[:, 0:128], kps, ["BbT"])
        S.op("dve", lambda e: e.memset(CT[:], 0.0), writes=["CT"])
        cpy("dve", CT[0:64, :, 0, 0:16], c_re[0:64], ks5, ["CT"])
        cpy("dve", CT[64:128, :, 0, 16:32], c_re[64:128], ks5, ["CT"])
        ts("dve", CT[0:64, :, 1, 0:16], c_im[0:64], -1.0, ALU.mult, ks5, ["CT"])
        ts("dve", CT[64:128, :, 1, 16:32], c_im[64:128], -1.0, ALU.mult, ks5, ["CT"])
        for m in range(4):
            ts("dve", Dd[:, m, :], ip32, vecs[:, V_DSK + m:V_DSK + m + 1], ALU.mult, ["cons", "vecs"], ["Dd"])
        rmag = mag

        def norm_mod(b, t0, n, ni, uT, ku, s_tmp):
            for s in range(n // 512):
                tsl = slice(t0 + s * 512, t0 + (s + 1) * 512)
                pss, kp = getps()
                for k in range(8):
                    sq, ksq = region(s_tmp, (k % 2) * 256, 256, BF16)
                    act(sq, hT[:, k, tsl], AF.Square, [("h", k, tsl.start // 512)], ksq)
                    mm(pss, onesD, sq, ksq + ["consb"], kp, start=(k == 0), stop=(k == 7))
                rstd, krs = region(s_tmp + 1, 0, 512)
                act(rstd, pss, AF.Ln, kp + ["s5vc"], krs, bias=epsc, scale=1.0)
                act(rstd, rstd, AF.Exp, krs, krs, scale=-0.5)
                for k in range(8):
                    tm, ktm = region(s_tmp + 2, (k % 2) * 512, 512)
                    tt("dve", tm, hT[:, k, tsl], rstd, ALU.mult, [("h", k, tsl.start // 512)] + krs, ktm)
                    act(uT[:, k, s * 512:(s + 1) * 512], tm, AF.Identity, ktm + ["GS", "modT"], ku,
                        bias=modT[:, (3 * ni) * 8 + k, b:b + 1], scale=GS[:, ni, k, b:b + 1])

        def ffn(b, t0, fi, ni):
            NT = 1024
            u, ku = region(0, 0, 4096, BF16)
            uT = u.rearrange("p (k t) -> p k t", t=NT)
            g, kg = region(4, 0, 11264, BF16)
            gT = g.rearrange("p (k t) -> p k t", t=NT)
            norm_mod(b, t0, NT, ni, uT, ku, 15)
            for m in range(22):
                wa, kwa = load_w(w1[fi], m, 1024)
                wb_, kwb = load_w(w3[fi], m, 1024)
                for s in range(2):
                    p1, k1 = getps()
                    p3, k3 = getps()
                    for k in range(8):
                        mm(p1, wa[:, k * 128:(k + 1) * 128], uT[:, k, s * 512:(s + 1) * 512], kwa + ku, k1,
                           start=(k == 0), stop=(k == 7))
                    for k in range(8):
                        mm(p3, wb_[:, k * 128:(k + 1) * 128], uT[:, k, s * 512:(s + 1) * 512], kwb + ku, k3,
                           start=(k == 0), stop=(k == 7))
                    sl_, ksl = region(18, ((2 * m + s) % 2) * 512, 512)
                    act(sl_, p1, AF.Silu, k1, ksl)
                    tt("dve", gT[:, m, s * 512:(s + 1) * 512], sl_, p3, ALU.mult, ksl + k3, kg)
                next(bg_tables, None)
            for m in range(8):
                wc, kwc = load_w(w2[fi], m, 2816)
                for s in range(2):
                    p, kp = getps()
                    for k in range(22):
                        mm(p, wc[:, k * 128:(k + 1) * 128], gT[:, k, s * 512:(s + 1) * 512], kwc + kg, kp,
                           start=(k == 0), stop=(k == 21))
                    hs = hT[:, m, t0 + s * 512:t0 + (s + 1) * 512]
                    stt("dve", hs, p, HG[:, ni, m, b:b + 1], hs, ALU.mult, ALU.add, kp + ["HG", ("h", m, s)], [("h", m, s)])

        def mixer(b, tt_i):
            t0 = (tt_i % 2) * 512
            tg0 = tt_i * 512
            NT = 512
            um, kum = region(0, 0, 2048, BF16)
            umT = um.rearrange("p (k t) -> p k t", t=NT)
            norm_mod(b, t0, NT, 1, umT, kum, 17)
            qTa, kqT = region(2, 0, 2048, BF16)
            kTa, kkT = region(4, 0, 2048, BF16)
            qT = qTa.rearrange("p (h t) -> p h t", t=NT)
            kT = kTa.rearrange("p (h t) -> p h t", t=NT)
            vta, kvt = region(6, 0, 1024, BF16)
            kta, kkt = region(7, 0, 1024, BF16)
            zsa, kzs = region(8, 0, 1024, BF16)
            vtok = vta.rearrange("p (c f) -> p c f", f=512)
            ktok = kta.rearrange("p (c f) -> p c f", f=512)
            zs = zsa.rearrange("p (c f) -> p c f", f=512)
            oga, kog = region(19, 0, 1024, BF16)
            ogT = oga.rearrange("p (m t) -> p m t", t=NT)
            KQ = [("qkpre", h_) for h_ in range(8)]
            KV = [("vpre", m_) for m_ in range(4)]
            if stage >= 2:
                if tt_i == 0:
                    S.op("pool", lambda e: e.memset(qkpre[:, :, 0:3], 0.0), writes=KQ)
                    S.op("pool", lambda e: e.memset(vpre[:, :, 0:3], 0.0), writes=KV)
                    S.op("pool", lambda e: e.memset(Sst[:], 0.0), writes=[("S", 0), ("S", 1)])
                    S.op("pool", lambda e: e.memset(Sbf[:], 0.0), writes=[("Sbf", 0), ("Sbf", 1)])
                else:
                    cpy("pool", qkpre[:, :, 0:3], qkpre[:, :, 512:515], KQ, KQ)
                    cpy("pool", vpre[:, :, 0:3], vpre[:, :, 512:515], KV, KV)
                for h in range(8):
                    wa, kwa = load_w(wq, h, 512)
                    wb_, kwb = load_w(wk, h, 512)
                    p, kp = getps()
                    for k in range(8):
                        mm(p[0:64, :], wa[:, k * 64:(k + 1) * 64], umT[:, k, :], kwa + kum, kp, start=(k == 0), stop=(k == 7))
                    for k in range(8):
                        mm(p[64:128, :], wb_[:, k * 64:(k + 1) * 64], umT[:, k, :], kwb + kum, kp, start=(k == 0),
                           stop=(k == 7), tp=(0, 64))
                    cpy("act", qkpre[:, h, 3:515], p, kp, [("qkpre", h)])
                for m in range(4):
                    wa, kwa = load_w(wv, m, 1024)
                    p, kp = getps()
                    for k in range(8):
                        mm(p, wa[:, k * 128:(k + 1) * 128], umT[:, k, :], kwa + kum, kp, start=(k == 0), stop=(k == 7))
                    cpy("act", vpre[:, m, 3:515], p, kp, [("vpre", m)])
                for h in range(8):
                    for which in range(2):
                        base = 64 * which
                        p, kp = getps()
                        for j in range(4):
                            mm(p[0:64, :], convqk[base:base + 64, h, j, :], qkpre[base:base + 64, h, j:j + 512],
                               [("qkpre", h), "convqk"], kp, start=(j == 0), stop=(j == 3), tp=(base, 0))
                        dstT = (qT if which == 0 else kT)
                        act(dstT[0:64, h, :], p[0:64, :], AF.Silu, kp, (kqT if which == 0 else kkT)[h:h + 1])
                for m in range(4):
                    p, kp = getps()
                    for j in range(4):
                        mm(p, convv[:, m, j, :], vpre[:, m, j:j + 512], [("vpre", m), "convv"], kp, start=(j == 0), stop=(j == 3))
                    vs, kvs = region(15 + (m % 2), 0, 256, BF16)
                    act(vs, p, AF.Silu, kp, kvs)
                    pt_, kpt = getps()
                    ptb = pt_.bitcast(BF16)
                    for c in range(4):
                        S.op("pe", lambda e, c=c, ptb=ptb, vs=vs: e.transpose(out=ptb[:, c * 128:(c + 1) * 128],
                                                                              in_=vs[:, c * 128:(c + 1) * 128], identity=identb),
                             reads=kvs + ["consb"], writes=kpt)
                    cpy("act", vtok[:, :, m * 128:(m + 1) * 128], ptb[:, 0:512].rearrange("p (c f) -> p c f", f=128), kpt, kvt)
                wz0, kz0 = load_w(wz, 0, 2048)
                wz1, kz1 = load_w(wz, 1, 2048)
                for c in range(4):
                    p, kp = getps()
                    for hf, (wzz, kzz) in enumerate(((wz0, kz0), (wz1, kz1))):
                        for k in range(8):
                            mm(p[:, hf * 256:(hf + 1) * 256], umT[:, k, c * 128:(c + 1) * 128], wzz[:, k * 256:(k + 1) * 256],
                               kum + kzz, kp, start=(k == 0), stop=(k == 7))
                    act(zs[:, c, :], p, AF.Silu, kp, kzs)
                for h in range(8):
                    for which in range(2):
                        dstT = (qT if which == 0 else kT)
                        kd = (kqT if which == 0 else kkT)[h:h + 1]
                        st_ = 9 + 2 * ((2 * h + which) % 3)
                        sq, ksq = region(st_ + 1, 512, 256, BF16, 64)
                        act(sq, dstT[0:64, h, :], AF.Square, kd, ksq)
                        p2, kp2 = getps()
                        mm(p2[0:64, :], bones[0:64, 0:64], sq, ksq + ["consb"], kp2)
                        rs, krs = region(st_ + 1, 0, 512, F32, 64)
                        if which == 0:
                            act(rs, p2[0:64, :], AF.Ln, kp2 + ["s5vc"], krs, bias=eps64[0:64], scale=64.0)
                        else:
                            act(rs, p2[0:64, :], AF.Ln, kp2 + ["s5vc"], krs, bias=epsc[0:64], scale=1.0)
                        act(rs, rs, AF.Exp, krs, krs, scale=-0.5)
                        tt("dve", dstT[0:64, h, :], dstT[0:64, h, :], rs, ALU.mult, kd + krs, kd)
                for hh in range(2):
                    pt_, kpt = getps()
                    ptb = pt_.bitcast(BF16)
                    for h4 in range(4):
                        h = hh * 4 + h4
                        for c in range(4):
                            o_ = ptb[:, (h4 * 4 + c) * 64:(h4 * 4 + c + 1) * 64]
                            S.op("pe", lambda e, o_=o_, h=h, c=c: e.transpose(out=o_, in_=kT[0:64, h, c * 128:(c + 1) * 128],
                                                                              identity=identb[0:64, 0:64]),
                                 reads=kkT + ["consb"], writes=kpt)
                    cpy("act", ktok[:, :, hh * 256:(hh + 1) * 256].rearrange("p c (h d) -> p h c d", d=64),
                        ptb.rearrange("p (h c d) -> p h c d", c=4, d=64), kpt, kkt)
                wb_, kwb = load_w(wba, 0, 128)
                pb, kpb = getps()
                for c in range(4):
                    for k in range(8):
                        mm(pb[:, c * 16:(c + 1) * 16], umT[:, k, c * 128:(c + 1) * 128], wb_[:, k * 16:(k + 1) * 16],
                           kum + kwb, kpb, start=(k == 0), stop=(k == 7))
                pb3 = pb[:, 0:64].rearrange("p (c f) -> p c f", f=16)
                beta = sm[:, 0:32].rearrange("p (c h) -> p c h", h=8)
                nbeta = sm[:, 32:64].rearrange("p (c h) -> p c h", h=8)
                la = sm[:, 64:96].rearrange("p (c h) -> p c h", h=8)
                gcum = sm[:, 96:128].rearrange("p (c h) -> p c h", h=8)
                KS = ["sm"]
                act(beta, pb3[:, :, 0:8], AF.Exp, kpb, KS, scale=-1.0)
                ts("dve", beta, beta, 1.0, ALU.add, KS, KS)
                S.op("dve", lambda e, beta=beta: e.reciprocal(out=beta, in_=beta), reads=KS, writes=KS)
                ts("dve", nbeta, beta, -1.0, ALU.mult, KS, KS)
                tt("dve", la, pb3[:, :, 8:16], vecs[:, V_DTB:V_DTB + 8].unsqueeze(1).broadcast_to([128, 4, 8]), ALU.add,
                   kpb + ["vecs"], KS)
                act(la, la, AF.Exp, KS, KS)
                act(la, la, AF.Ln, KS + ["s5vc"], KS, bias=onec, scale=1.0)
                tt("dve", la, la, negexpA.unsqueeze(1).broadcast_to([128, 4, 8]), ALU.mult, KS + ["negexpA"], KS)
                pg, kpg = getps()
                mm(pg[:, 0:32], tri, sm[:, 64:96], KC + KS, kpg)
                cpy("dve", sm[:, 96:128], pg[:, 0:32], kpg, KS)

                def chunk_gen(c, g):
                    csl = slice(c * 128, (c + 1) * 128)
                    hs = slice(4 * g, 4 * g + 4)
                    A_dg, k_dg = region(9, g * 512, 512)
                    A_dt, k_dt = region(10, g * 512, 512)
                    A_eg, k_eg = region(11, g * 512, 512)
                    A_m2, k_m2 = region(12, g * 512, 512)
                    A_at, k_at = region(13, g * 256, 256, BF16)
                    A_ek, k_ek = region(13, 512 + g * 256, 256, BF16, 64)
                    A_qd, k_qd = region(14, g * 256, 256, BF16, 64)
                    A_kd, k_kd = region(14, 512 + g * 128, 128, BF16)
                    A_R, k_R = region(14, 768 + g * 128, 128, BF16)
                    A_vn, k_vn = vnb[:, g * 256:(g + 1) * 256], [("vnb", g)]
                    A_P = [region(15, i_ * 512 + g * 256, 256, BF16) for i_ in range(2)]
                    A_Q = [region(16, i_ * 512 + g * 256, 256, BF16) for i_ in range(2)]
                    A_T = [region(17, i_ * 512 + g * 256, 256, BF16) for i_ in range(2)]
                    A_of, k_of = region(18, g * 256, 256)
                    A_o2, k_o2 = region(18, 512 + g * 256, 256)
                    kkTg = kkT[4 * g:4 * g + 4]
                    kqTg = kqT[4 * g:4 * g + 4]
                    KSg = KS
                    kSt, kSb = [("S", g)], [("Sbf", g)]
                    Sst_g = Sst[:, g * 256:(g + 1) * 256]
                    Sbf_g = Sbf[:, g * 256:(g + 1) * 256]

                    def v3(ap):
                        return ap.rearrange("p (h i) -> p h i", h=4)

                    def b4(ap2d):
                        return ap2d.unsqueeze(1).broadcast_to([128, 4, 128])
                    gc = gcum[:, c, hs]
                    gcb = gc.unsqueeze(2).broadcast_to([128, 4, 128])
                    tt("dve", v3(A_dg), b4(ident), gcb, ALU.mult, KC + KSg, k_dg)
                    pG, kG = getps()
                    mm(pG, onesf, A_dg, KC + k_dg, kG)
                    yield
                    tt("dve", v3(A_dt), v3(pG), gcb, ALU.subtract, kG + KSg, k_dt)
                    stt("dve", v3(A_dt), v3(A_dt), 0.0, b4(negm), ALU.min, ALU.add, k_dt + KC, k_dt)
                    act(A_dt, A_dt, AF.Exp, k_dt, k_dt)
                    act(A_eg, pG, AF.Exp, kG, k_eg)
                    edl = sm[:, 128 + 4 * g:132 + 4 * g]
                    kedl = [("edl", g)]
                    tt("dve", edl, v3(pG)[:, :, 127], gc, ALU.subtract, kG + KSg, kedl)
                    act(edl, edl, AF.Exp, kedl, kedl)
                    pK, kK = getps()
                    pQ, kQ_ = getps()
                    for hh in range(4):
                        h = 4 * g + hh
                        mm(pK[:, hh * 128:(hh + 1) * 128], kT[0:64, h, csl], kT[0:64, h, csl], kkTg, kK)
                    for hh in range(4):
                        h = 4 * g + hh
                        mm(pQ[:, hh * 128:(hh + 1) * 128], kT[0:64, h, csl], qT[0:64, h, csl], kkTg + kqTg, kQ_)
                    yield
                    tt("dve", A_at, pQ, A_dt, ALU.mult, kQ_ + k_dt, k_at)
                    tt("dve", v3(A_m2), b4(strict), nbeta[:, c, hs].unsqueeze(2).broadcast_to([128, 4, 128]), ALU.mult,
                       KC + KSg, k_m2)
                    tt("dve", A_dg, pK, A_dt, ALU.mult, kK + k_dt, k_dg)
                    P0, kP0 = A_P[0]
                    tt("dve", P0, A_dg, A_m2, ALU.mult, k_dg + k_m2, kP0)
                    pT_, kT_ = getps()
                    pTb = pT_.bitcast(BF16)
                    for hh in range(4):
                        S.op("pe", lambda e, hh=hh, pTb=pTb, P0=P0: e.transpose(out=pTb[:, hh * 128:(hh + 1) * 128],
                                                                                in_=P0[:, hh * 128:(hh + 1) * 128], identity=identb),
                             reads=kP0 + ["consb"], writes=kT_)
                    Q0, kQ0 = A_Q[0]
                    cpy("act", Q0, pTb[:, 0:512], kT_, kQ0)
                    yield
                    M_, kM = A_P[0]
                    MT_, kMT = A_Q[0]
                    B1, kB1 = region(9, g * 512, 256, BF16)
                    B2, kB2 = region(9, g * 512 + 256, 256, BF16)
                    B3, kB3 = region(12, g * 512, 256, BF16)
                    B4, kB4 = region(12, g * 512 + 256, 256, BF16)
                    Md, kMd = A_P[1]
                    MdT, kMdT = A_Q[1]

                    def mk(i_):
                        return b4(masks[:, i_ * 128:(i_ + 1) * 128])

                    def grp(lh, rh, rk):
                        pz, kz = getps()
                        for hh in range(4):
                            hs_ = slice(hh * 128, (hh + 1) * 128)
                            mm(pz[:, hs_], lh[:, hs_], rh[:, hs_], rk, kz)
                        return pz, kz
                    tt("dve", v3(Md), v3(M_), mk(0), ALU.mult, kM + ["masks"], kMd)
                    tt("dve", v3(MdT), v3(MT_), mk(0), ALU.mult, kMT + ["masks"], kMdT)
                    pz, kz = grp(MdT, Md, kMd + kMdT)
                    cpy("act", B1, pz, kz, kB1)
                    pz, kz = grp(Md, MdT, kMd + kMdT)
                    cpy("act", B2, pz, kz, kB2)
                    Ta, kTa = A_T[0]
                    TTa, kTTa = A_T[1]
                    idb = b4(identb)
                    tt("dve", v3(Ta), v3(Md), idb, ALU.add, kMd + ["consb"], kTa)
                    tt("dve", v3(TTa), v3(MdT), idb, ALU.add, kMdT + ["consb"], kTTa)
                    yield
                    pz, kz = grp(B2, B1, kB1 + kB2)
                    cpy("act", B3, pz, kz, kB3)
                    pz, kz = grp(B1, B2, kB1 + kB2)
                    cpy("act", B4, pz, kz, kB4)
                    pz, kz = grp(B2, Ta, kB2 + kTa)
                    tt("dve", Md, Ta, pz, ALU.add, kTa + kz, kMd)
                    pz, kz = grp(B1, TTa, kB1 + kTTa)
                    tt("dve", MdT, TTa, pz, ALU.add, kTTa + kz, kMdT)
                    yield
                    pz, kz = grp(B4, Md, kB4 + kMd)
                    tt("dve", Ta, Md, pz, ALU.add, kMd + kz, kTa)
                    pz, kz = grp(B3, MdT, kB3 + kMdT)
                    tt("dve", TTa, MdT, pz, ALU.add, kMdT + kz, kTTa)
                    yield
                    Tc, kTc, TTc, kTTc = Ta, kTa, TTa, kTTa
                    Tn, kTn, TTn, kTTn = Md, kMd, MdT, kMdT
                    for li in range(4):
                        last = (li == 3)
                        pz, kz = grp(MT_, Tc, kMT + kTc)
                        cpy("act", B1, pz, kz, kB1)
                        if not last:
                            pz2, kz2 = grp(M_, TTc, kM + kTTc)
                            cpy("act", B2, pz2, kz2, kB2)
                        yield
                        pz, kz = grp(TTc, B1, kTTc + kB1)
                        tt("dve", v3(B3), v3(pz), mk(1 + 2 * li), ALU.mult, kz + ["masks"], kB3)
                        tt("dve", Tn, Tc, B3, ALU.add, kTc + kB3, kTn)
                        if not last:
                            pz2, kz2 = grp(Tc, B2, kTc + kB2)
                            tt("dve", v3(B4), v3(pz2), mk(2 + 2 * li), ALU.mult, kz2 + ["masks"], kB4)
                            tt("dve", TTn, TTc, B4, ALU.add, kTTc + kB4, kTTn)
                        Tc, kTc, Tn, kTn = Tn, kTn, Tc, kTc
                        TTc, kTTc, TTn, kTTn = TTn, kTTn, TTc, kTTc
                        yield
                    Tt, kTt = Tc, kTc
                    tt("dve", A_ek, kT[0:64, hs, csl], v3(A_eg)[0:64], ALU.mult, kkTg + k_eg, k_ek)
                    tt("dve", A_qd, qT[0:64, hs, csl], v3(A_eg)[0:64], ALU.mult, kqTg + k_eg, k_qd)
                    tt("dve", A_kd.rearrange("p (h d) -> p h d", d=64),
                       ktok[:, c, g * 256:(g + 1) * 256].rearrange("p (h d) -> p h d", d=64),
                       edl.unsqueeze(2).broadcast_to([128, 4, 64]), ALU.mult, kkt + kedl, k_kd)
                    ek3 = v3(A_ek)
                    qd3 = v3(A_qd)
                    pY, kY = getps()
                    for hh in range(4):
                        mm(pY[:, hh * 64:(hh + 1) * 64], ek3[:, hh, :], Sbf_g[:, hh * 64:(hh + 1) * 64], k_ek + kSb, kY)
                    yield
                    tt("dve", A_R, vtok[:, c, g * 256:(g + 1) * 256], pY[:, 0:256], ALU.subtract, kvt + kY, k_R)
                    pX, kX = getps()
                    for hh in range(4):
                        mm(pX[:, hh * 64:(hh + 1) * 64], Tt[:, hh * 128:(hh + 1) * 128], A_R[:, hh * 64:(hh + 1) * 64], kTt + k_R, kX)
                    yield
                    tt("dve", A_vn.rearrange("p (h d) -> p h d", d=64), pX[:, 0:256].rearrange("p (h d) -> p h d", d=64),
                       beta[:, c, hs].unsqueeze(2).broadcast_to([128, 4, 64]), ALU.mult, kX + KSg, k_vn)
                    pO, kO = getps()
                    for hh in range(4):
                        mm(pO[:, hh * 64:(hh + 1) * 64], qd3[:, hh, :], Sbf_g[:, hh * 64:(hh + 1) * 64], k_qd + kSb, kO,
                           start=True, stop=False)
                        mm(pO[:, hh * 64:(hh + 1) * 64], A_at[:, hh * 128:(hh + 1) * 128], A_vn[:, hh * 64:(hh + 1) * 64],
                           k_at + k_vn, kO, start=False, stop=True)
                    pS, kS_ = getps()
                    for hh in range(4):
                        mm(pS[0:64, hh * 64:(hh + 1) * 64], A_kd[:, hh * 64:(hh + 1) * 64], A_vn[:, hh * 64:(hh + 1) * 64],
                           k_kd + k_vn, kS_)
                    yield
                    S3 = Sst_g.rearrange("p (h d) -> p h d", d=64)
                    tt("dve", S3, S3, v3(A_eg)[0:64, :, 127:128].broadcast_to([64, 4, 64]), ALU.mult, kSt + k_eg, kSt)
                    tt("dve", Sst_g, Sst_g, pS[0:64, 0:256], ALU.add, kSt + kS_, kSt)
                    cpy("act", Sbf_g, Sst_g, kSt, kSb)
                    cpy("act", A_of, pO[:, 0:256], kO, k_of)
                    yield
                    tt("dve", A_o2, A_of, A_of, ALU.mult, k_of, k_o2)
                    ss8 = sm[:, 136 + 4 * g:140 + 4 * g]
                    kss = [("ss8", g)]
                    S.op("dve", lambda e, A_o2=A_o2, ss8=ss8: e.tensor_reduce(out=ss8, in_=A_o2.rearrange("p (h d) -> p h d", d=64),
                                                                              axis=AX.X, op=ALU.add), reads=k_o2, writes=kss)
                    act(ss8, ss8, AF.Ln, kss + ["s5vc"], kss, bias=epsc, scale=1.0 / 64)
                    act(ss8, ss8, AF.Exp, kss, kss, scale=-0.5)
                    tt("dve", A_o2.rearrange("p (h d) -> p h d", d=64),
                       zs[:, c, g * 256:(g + 1) * 256].rearrange("p (h d) -> p h d", d=64),
                       vecs[:, V_GON:V_GON + 64].unsqueeze(1).broadcast_to([128, 4, 64]), ALU.mult, kzs + ["vecs"], k_o2)
                    yield
                    tt("dve", A_of.rearrange("p (h d) -> p h d", d=64), A_of.rearrange("p (h d) -> p h d", d=64),
                       ss8.unsqueeze(2).broadcast_to([128, 4, 64]), ALU.mult, k_of + kss, k_of)
                    og, kogc = region(13, g * 256, 128, BF16)
                    tt("dve", og, A_of, A_o2, ALU.mult, k_of + k_o2, kogc)
                    pT2, kT2 = getps()
                    pT2b = pT2.bitcast(BF16)
                    for mm_ in range(2):
                        S.op("pe", lambda e, mm_=mm_, pT2b=pT2b, og=og: e.transpose(out=pT2b[:, mm_ * 128:(mm_ + 1) * 128],
                                                                                    in_=og[:, mm_ * 128:(mm_ + 1) * 128], identity=identb),
                             reads=kogc + ["consb"], writes=kT2)
                    cpy("act", ogT[:, 2 * g:2 * g + 2, csl], pT2b[:, 0:256].rearrange("p (m t) -> p m t", t=128), kT2,
                        kog[2 * g:2 * g + 2])
                    yield

                for c in range(4):
                    gens = [chunk_gen(c, 0), chunk_gen(c, 1)]
                    alive = [True, True]
                    while any(alive):
                        for gi in range(2):
                            if alive[gi]:
                                try:
                                    next(gens[gi])
                                except StopIteration:
                                    alive[gi] = False
            else:
                S.op("pool", lambda e: e.memset(oga, 0.0), writes=kog)

            ybha, kyb = region(8, 0, 1024, BF16)
            ybh = ybha.rearrange("p (m t) -> p m t", t=NT)
            if stage >= 3:
                s5ua, ks5u = region(9, 0, 1024, BF16)
                s5u = s5ua.rearrange("p (m t) -> p m t", t=NT)
                yga, kyg = region(10, 0, 1024, BF16)
                yg = yga.rearrange("p (m t) -> p m t", t=NT)
                for m in range(4):
                    wa, kwa = load_w(ws5, m, 1024)
                    p, kp = getps()
                    for k in range(8):
                        mm(p, wa[:, k * 128:(k + 1) * 128], umT[:, k, :], kwa + kum, kp, start=(k == 0), stop=(k == 7))
                    cpy("act", s5u[:, m, :], p, kp, ks5u)
                if tt_i == 0:
                    S.op("pool", lambda e: e.memset(zst[:], 0.0), writes=[("zst", p_) for p_ in range(16)])
                def s5_front(pt):
                    m, q = pt // 4, pt % 4
                    par = pt % 2
                    cosT, kco = region(11, par * 256, 256, BF16)
                    sinT, ksi = region(12, par * 256, 256, BF16)
                    S.dma("sp", lambda e, cosT=cosT, pt=pt: e.dma_start(out=cosT, in_=tab_d[pt, 0, :, tg0:tg0 + 512]),
                          reads=[("tab", pt)], writes=kco)
                    S.dma("sp", lambda e, sinT=sinT, pt=pt: e.dma_start(out=sinT, in_=tab_d[pt, 1, :, tg0:tg0 + 512]),
                          reads=[("tab", pt)], writes=ksi)
                    bi_ = (2 * pt) % 6
                    pbr, kbr = psum[:, bi_ * 512:(bi_ + 1) * 512], [("ps", bi_)]
                    pbi, kbi = psum[:, (bi_ + 1) * 512:(bi_ + 2) * 512], [("ps", bi_ + 1)]
                    rows = slice(32 * q, 32 * q + 32)
                    mm(pbr, BbT[rows, m, 0, :], s5u[rows, m, :], ["BbT"] + ks5u, kbr, tp=(32 * q, 0))
                    mm(pbi, BbT[rows, m, 1, :], s5u[rows, m, :], ["BbT"] + ks5u, kbi, tp=(32 * q, 0))
                    return cosT, kco, sinT, ksi, pbr, kbr, pbi, kbi

                def s5_back(pt, fr):
                    cosT, kco, sinT, ksi, pbr, kbr, pbi, kbi = fr
                    m, q = pt // 4, pt % 4
                    par = pt % 2
                    rows = slice(32 * q, 32 * q + 32)
                    py, kpy = psum[:, (6 + m % 2) * 512:(7 + m % 2) * 512], [("ps", 6 + m % 2)]
                    sA, sB, sC, sD = ((13, 14, 15, 16) if par == 0 else (20, 21, 22, 23))
                    m1, km1 = region(sA, 0, 256, BF16)
                    m2, km2 = region(sA, 256, 256, BF16)
                    m3, km3 = region(sA, 512, 256, BF16)
                    m4, km4 = region(sA, 768, 256, BF16)
                    brb, kbrb = region(sB, 0, 256, BF16)
                    bib, kbib = region(sB, 256, 256, BF16)
                    cre2, kcr = region(sB, 512, 256, BF16)
                    cim2, kci = region(sB, 768, 256, BF16)
                    zre, kzr = region(sC, 0, 512)
                    zim, kzi = region(sC, 512, 512)
                    zrb, kzrb = region(sD, 0, 256, BF16)
                    zib, kzib = region(sD, 256, 256, BF16)
                    cpy("act", brb, pbr, kbr, kbrb)
                    cpy("act", bib, pbi, kbi, kbib)
                    tt("dve", m1, brb, cosT, ALU.mult, kbrb + kco, km1)
                    tt("dve", m2, bib, sinT, ALU.mult, kbib + ksi, km2)
                    tt("dve", m3, bib, cosT, ALU.mult, kbib + kco, km3)
                    tt("dve", m4, brb, sinT, ALU.mult, kbrb + ksi, km4)
                    tt("dve", cre2, m1, m2, ALU.add, km1 + km2, kcr)
                    tt("dve", cim2, m3, m4, ALU.subtract, km3 + km4, kci)
                    rb = rmag[:, pt:pt + 1].broadcast_to([128, 512])
                    S.op("dve", lambda e: e.tensor_tensor_scan(
                        out=zre, data0=rb, data1=cre2, initial=zst[:, pt, 0:1], op0=ALU.mult, op1=ALU.add),
                        reads=kcr + K5 + [("zst", pt)], writes=kzr)
                    cpy("act", zrb, zre, kzr, kzrb)
                    S.op("dve", lambda e: e.tensor_tensor_scan(
                        out=zim, data0=rb, data1=cim2, initial=zst[:, pt, 1:2], op0=ALU.mult, op1=ALU.add),
                        reads=kci + K5 + [("zst", pt)], writes=kzi)
                    cpy("act", zib, zim, kzi, kzib)
                    cpy("act", zst[:, pt, 0:1], zre[:, 511:512], kzr, [("zst", pt)])
                    cpy("act", zst[:, pt, 1:2], zim[:, 511:512], kzi, [("zst", pt)])
                    tt("dve", m1, zrb, cosT, ALU.mult, kzrb + kco, km1)
                    tt("dve", m3, zrb, sinT, ALU.mult, kzrb + ksi, km3)
                    tt("dve", m2, zib, sinT, ALU.mult, kzib + ksi, km2)
                    tt("dve", m4, zib, cosT, ALU.mult, kzib + kco, km4)
                    xre, kxr = region(sD, 512, 256, BF16)
                    xim, kxi = region(sD, 768, 256, BF16)
                    tt("dve", xre, m1, m2, ALU.subtract, km1 + km2, kxr)
                    tt("dve", xim, m3, m4, ALU.add, km3 + km4, kxi)
                    mm(py[rows, :], CT[:, pt, 0, :], xre, ["CT"] + kxr, kpy, start=True, stop=False, tp=(0, 32 * q))
                    mm(py[rows, :], CT[:, pt, 1, :], xim, ["CT"] + kxi, kpy, start=False, stop=False, tp=(0, 32 * q))
                    mm(py[rows, :], Dd[rows, m, :], s5u[rows, m, :], ["Dd"] + ks5u, kpy, start=False, stop=True,
                       tp=(32 * q, 32 * q))
                    if q == 3:
                        act(yg[:, m, :], py, AF.Gelu, kpy, kyg)

                fr_next = s5_front(0)
                for pt in range(16):
                    fr_cur = fr_next
                    if pt + 1 < 16:
                        fr_next = s5_front(pt + 1)
                    s5_back(pt, fr_cur)
                for m in range(4):
                    wa, kwa = load_w(wglu, m, 512)
                    p, kp = getps()
                    for k in range(4):
                        mm(p, wa[:, k * 128:(k + 1) * 128], yg[:, k, :], kwa + kyg, kp, start=(k == 0), stop=(k == 3))
                    sg, ksg = region(13 + (m % 2), 0, 512)
                    act(sg, p, AF.Sigmoid, kp + ["vecs"], ksg, bias=vecs[:, V_BGLU + m:V_BGLU + m + 1], scale=1.0)
                    tt("dve", ybh[:, m, :], yg[:, m, :], sg, ALU.mult, kyg + ksg, kyb)
            else:
                S.op("pool", lambda e: e.memset(ybha, 0.0), writes=kyb)

            mga, kmg = region(2, 0, 2048, BF16)
            mg = mga.rearrange("p (m t) -> p m t", t=NT)
            for m in range(8):
                wa, kwa = load_w(wpa, m, 512)
                wb_, kwb = load_w(wpb, m, 512)
                wc, kwc = load_w(wga, m, 1024)
                wd, kwd = load_w(wgb, m, 1024)
                pa, kpa = getps()
                pb_, kpb_ = getps()
                pga, kpga = getps()
                pgb, kpgb = getps()
                for k in range(4):
                    mm(pa, wa[:, k * 128:(k + 1) * 128], ogT[:, k, :], kwa + kog, kpa, start=(k == 0), stop=(k == 3))
                for k in range(4):
                    mm(pb_, wb_[:, k * 128:(k + 1) * 128], ybh[:, k, :], kwb + kyb, kpb_, start=(k == 0), stop=(k == 3))
                for k in range(8):
                    mm(pga, wc[:, k * 128:(k + 1) * 128], umT[:, k, :], kwc + kum, kpga, start=(k == 0), stop=(k == 7))
                for k in range(8):
                    mm(pgb, wd[:, k * 128:(k + 1) * 128], umT[:, k, :], kwd + kum, kpgb, start=(k == 0), stop=(k == 7))
                sa, ksa = region(9 + (m % 2), 0, 512)
                sb_, ksb = region(9 + (m % 2), 512, 512)
                act(sa, pga, AF.Sigmoid, kpga, ksa)
                act(sb_, pgb, AF.Sigmoid, kpgb, ksb)
                tt("dve", sa, sa, pa, ALU.mult, ksa + kpa, ksa)
                tt("dve", sb_, sb_, pb_, ALU.mult, ksb + kpb_, ksb)
                tt("dve", mg[:, m, :], sa, sb_, ALU.add, ksa + ksb, kmg)
            for m in range(8):
                wa, kwa = load_w(wout, m, 1024)
                p, kp = getps()
                for k in range(8):
                    mm(p, wa[:, k * 128:(k + 1) * 128], mg[:, k, :], kwa + kmg, kp, start=(k == 0), stop=(k == 7))
                hs = hT[:, m, t0:t0 + 512]
                stt("dve", hs, p, HG[:, 1, m, b:b + 1], hs, ALU.mult, ALU.add, kp + ["HG", ("h", m, t0 // 512)], [("h", m, t0 // 512)])

        def final_out(b, t0, n, g0):
            for s in range(n // 512):
                tsl = slice(t0 + s * 512, t0 + (s + 1) * 512)
                pss, kp = getps()
                for k in range(8):
                    sq, ksq = region(15, (k % 2) * 256, 256, BF16)
                    act(sq, hT[:, k, tsl], AF.Square, [("h", k, tsl.start // 512)], ksq)
                    mm(pss, onesD, sq, ksq + ["consb"], kp, start=(k == 0), stop=(k == 7))
                rstd, krs = region(16, 0, 512)
                act(rstd, pss, AF.Ln, kp + ["s5vc"], krs, bias=epsc, scale=1.0)
                act(rstd, rstd, AF.Exp, krs, krs, scale=-0.5)
                ob, kob = region(0 + 4 * (s % 2), 0, 4096)
                ob3 = ob.rearrange("p (k t) -> p k t", t=512)
                for k in range(8):
                    tm, ktm = region(17, (k % 2) * 512, 512)
                    tt("dve", tm, hT[:, k, tsl], rstd, ALU.mult, [("h", k, tsl.start // 512)] + krs, ktm)
                    act(ob3[:, k, :], tm, AF.Copy, ktm + ["vecs"], kob, scale=vecs[:, V_GF + k:V_GF + k + 1])
                gsl = slice(g0 + t0 + s * 512, g0 + t0 + (s + 1) * 512)
                S.dma("sp", lambda e, ob3=ob3, gsl=gsl, b=b: e.dma_start(out=oT[b, :, :, gsl], in_=ob3), reads=kob,
                      writes=["out"])

        for b in range(nseq):
            for hf in range(2):
                for k in range(8):
                    S.dma("sp", lambda e, b=b, k=k, hf=hf: e.dma_start(out=hT[:, k, :], in_=xT[b, :, k, hf * 1024:(hf + 1) * 1024]),
                          writes=[("h", k, 0), ("h", k, 1)])
                if stage >= 1:
                    ffn(b, 0, 0, 0)
                for _ in bg_tables:
                    pass
                if stage >= 2:
                    for t in range(2):
                        mixer(b, hf * 2 + t)
                if stage >= 4:
                    ffn(b, 0, 1, 2)
                final_out(b, 0, 1024, hf * 1024)
        S.final_wait("sp", ["out"])
        S.run(block)
    return nc, S.ninst


def tile_w(W, Mw):
    K, M = W.shape
    return np.ascontiguousarray(W.reshape(K // 128, 128, M // Mw, Mw).transpose(2, 1, 0, 3)).reshape(M // Mw, 128, (K // 128) * Mw)


def prep_shared(I):
    f = np.float32
    sh = {}
    sh["wada"] = tile_w(I["w_ada"][0], 128)
    sh["bada"] = np.ascontiguousarray(I["b_ada"][0].reshape(72, 128).T)
    for nm, a, bname in (("w1", "w1_ffn1", "w1_ffn2"), ("w3", "w3_ffn1", "w3_ffn2"), ("w2", "w2_ffn1", "w2_ffn2")):
        sh[nm + "a"] = tile_w(I[a][0], 128)
        sh[nm + "b"] = tile_w(I[bname][0], 128)
    Win = I["w_in"][0]
    sh["wq"] = tile_w(Win[:, 0:512], 64)
    sh["wk"] = tile_w(Win[:, 512:1024], 64)
    sh["wv"] = tile_w(Win[:, 1024:1536], 128)
    sh["wz"] = tile_w(Win[:, 1536:2048], 256)
    sh["wba"] = tile_w(Win[:, 2048:2064], 16)
    sh["ws5"] = tile_w(Win[:, 2064:2576], 128)
    sh["wga"] = tile_w(Win[:, 2576:3600], 128)
    sh["wgb"] = tile_w(Win[:, 3600:4624], 128)
    sh["wpa"] = tile_w(I["w_proj_a"][0], 128)
    sh["wpb"] = tile_w(I["w_proj_b"][0], 128)
    sh["wglu"] = tile_w(I["w_glu"][0], 128)
    sh["wout"] = tile_w(I["w_out"][0], 128)
    vecs = np.zeros((128, NV), f)

    def col8(v):
        return v.reshape(-1, 128).T
    vecs[:, V_G1:V_G1 + 8] = col8(I["g_ffn1"][0])
    vecs[:, V_GM:V_GM + 8] = col8(I["g_mix"][0])
    vecs[:, V_G2:V_G2 + 8] = col8(I["g_ffn2"][0])
    vecs[:, V_GF:V_GF + 8] = col8(I["g_final"])
    vecs[:, V_BGLU:V_BGLU + 4] = col8(I["b_glu"][0])
    vecs[:, V_DSK:V_DSK + 4] = col8(I["d_skip"][0])
    conv = I["conv_qkv"][0]
    cq = conv[:, 0:512].T.reshape(8, 64, 4).transpose(1, 0, 2).reshape(64, 32)
    ck = conv[:, 512:1024].T.reshape(8, 64, 4).transpose(1, 0, 2).reshape(64, 32)
    vecs[0:64, V_CQK:V_CQK + 32] = cq
    vecs[64:128, V_CQK:V_CQK + 32] = ck
    vecs[:, V_CV:V_CV + 16] = conv[:, 1024:1536].T.reshape(4, 128, 4).transpose(1, 0, 2).reshape(128, 16)

    def st(a):
        return a.reshape(16, 2, 64).transpose(1, 2, 0).reshape(128, 16)
    vecs[:, V_LRE:V_LRE + 16] = st(I["lam_re"][0])
    vecs[:, V_LIM:V_LIM + 16] = st(I["lam_im"][0])
    vecs[:, V_LST:V_LST + 16] = st(np.broadcast_to(I["log_step"][0][:, None], (32, 64)))
    vecs[:, V_ALOG:V_ALOG + 8] = np.broadcast_to(I["a_log"][0][None, :], (128, 8))
    vecs[:, V_DTB:V_DTB + 8] = np.broadcast_to(I["dt_bias"][0][None, :], (128, 8))
    vecs[:, V_GON:V_GON + 64] = np.broadcast_to(I["g_onorm"][0][None, :], (128, 64))
    sh["vecs"] = vecs
    cons = np.zeros((128, NCON), f)
    jj, ii = np.meshgrid(np.arange(128), np.arange(128), indexing="ij")
    cons[:, C_ID:C_ID + 128] = np.eye(128)
    cons[:, C_TRI:C_TRI + 128] = (jj <= ii)
    cons[:, C_NEG:C_NEG + 128] = np.where(jj <= ii, 0.0, -1e4)
    cons[:, C_STR:C_STR + 128] = (jj < ii)
    cons[:, C_ONE:C_ONE + 128] = 1.0
    cons[:, C_IP32:C_IP32 + 32] = (np.arange(128)[:, None] % 32 == np.arange(32)[None, :])
    sh["cons"] = cons
    mk = np.zeros((128, 9 * 128), f)
    mk[:, 0:128] = (jj // 8 == ii // 8)
    for li, bsz in enumerate((8, 16, 32, 64)):
        mb = (jj // (2 * bsz) == ii // (2 * bsz)) & (jj % (2 * bsz) < bsz) & (ii % (2 * bsz) >= bsz)
        mk[:, (1 + 2 * li) * 128:(2 + 2 * li) * 128] = mb
        mk[:, (2 + 2 * li) * 128:(3 + 2 * li) * 128] = mb.T
    sh["masks"] = mk

    def bl(a):
        return a.reshape(16, 2, 64, 16).transpose(1, 2, 0, 3).reshape(128, 256)

    def cl(a):
        return a.reshape(16, 2, 16, 64).transpose(1, 3, 0, 2).reshape(128, 256)
    sh["s5bc"] = np.concatenate([bl(I["b_re"][0]), bl(I["b_im"][0]), cl(I["c_re"][0]), cl(I["c_im"][0])], axis=1)
    return {k: np.ascontiguousarray(v, dtype=f) for k, v in sh.items()}


def prep_core(I, core, nseq=4):
    x = I["x"][core * 4:core * 4 + nseq]
    xT = np.ascontiguousarray(x.transpose(0, 2, 1).reshape(nseq, 8, 128, SEQ).transpose(0, 2, 1, 3))
    c = I["c"][core * 4:core * 4 + 4]
    cT = np.ascontiguousarray(c.T.reshape(8, 128, 4).transpose(1, 0, 2))
    return {"xT": xT.astype(np.float32), "cT": cT.astype(np.float32)}


_CACHE = {}


def kernel(**inputs):
    I = {k: np.asarray(v) for k, v in inputs.items()}
    if "nc" not in _CACHE:
        _CACHE["nc"] = build(4)[0]
    nc = _CACHE["nc"]
    sh = prep_shared(I)
    in_maps = []
    for c in range(NCORE):
        d = dict(sh)
        d.update(prep_core(I, c))
        in_maps.append(d)
    res = run_bass_kernel_spmd(nc, in_maps, core_ids=list(range(NCORE)))
    out = np.empty((32, SEQ, D), np.float32)
    for c in range(NCORE):
        o = res.results[c]["oT"]
        out[c * 4:(c + 1) * 4] = o.transpose(0, 3, 2, 1).reshape(4, SEQ, D)
    return out
```

```python
import os
import numpy as np
from contextlib import ExitStack
import concourse.bass as bass
import concourse.mybir as mybir
from concourse.bass_utils import run_bass_kernel_spmd

F32 = mybir.dt.float32
BF16 = mybir.dt.bfloat16
AF = mybir.ActivationFunctionType
ALU = mybir.AluOpType
AX = mybir.AxisListType

D = 1024
SEQ = 2048
DFF = 2816
NCORE = 8
EPS = 1e-6
NSLOT = 26
SLOT = 1024
WBN = 4
WBEL = 3072


class Sched:
    COMPUTE = ("pe", "act", "dve", "pool")

    def __init__(self, nc, es, n_dma_sems=24):
        self.nc = nc
        self.sem = {}
        self.cnt = {}
        for e in self.COMPUTE:
            self.sem[e] = es.enter_context(nc.semaphore("s_" + e))
            self.cnt[e] = 0
        self.dsem = [es.enter_context(nc.semaphore("d%d" % i)) for i in range(n_dma_sems)]
        self.dcnt = [0] * n_dma_sems
        self.dnext = 0
        self.dq = {}
        self.seen = {e: {} for e in ("pe", "act", "dve", "pool", "sp")}
        self.last_w = {}
        self.readers = {}
        self.prog = {e: [] for e in ("pe", "act", "dve", "pool", "sp")}
        self.ninst = 0

    def _deps(self, eng, reads, writes):
        deps = {}

        def add(tok):
            s, v = tok
            if deps.get(s, 0) < v:
                deps[s] = v
        for k in reads:
            if k in self.last_w:
                add(self.last_w[k])
        for k in writes:
            if k in self.last_w:
                add(self.last_w[k])
            for t in self.readers.get(k, ()):
                add(t)
        out = []
        for s, v in deps.items():
            if eng == "pe" and s is self.sem["pe"]:
                continue
            if self.seen[eng].get(s, 0) >= v:
                continue
            self.seen[eng][s] = v
            out.append((s, v))
        return out

    def _record(self, tok, reads, writes):
        for k in reads:
            self.readers.setdefault(k, []).append(tok)
        for k in writes:
            self.last_w[k] = tok
            self.readers[k] = []

    def op(self, eng, fn, reads=(), writes=()):
        waits = self._deps(eng, reads, writes)
        self.cnt[eng] += 1
        tok = (self.sem[eng], self.cnt[eng])
        sem = self.sem[eng]

        def emit(e, fn=fn, waits=waits, sem=sem):
            for s, v in waits:
                e.wait_ge(s, v)
            fn(e).then_inc(sem, 1)
        self.prog[eng].append(emit)
        self._record(tok, reads, writes)
        self.ninst += 1

    def dma(self, q, fn, reads=(), writes=()):
        n = len(self.dsem)
        lo, hi = (0, (2 * n) // 3) if q == "sp" else ((2 * n) // 3, n)
        cur = self.dq.get(q, lo)
        i = cur
        self.dq[q] = lo + ((cur + 1 - lo) % (hi - lo))
        s = self.dsem[i]
        waits = self._deps(q, reads, writes)
        if self.dcnt[i] > 0 and self.seen[q].get(s, 0) < self.dcnt[i]:
            waits.append((s, self.dcnt[i]))
            self.seen[q][s] = self.dcnt[i]
        self.dcnt[i] += 16
        tok = (s, self.dcnt[i])

        def emit(e, fn=fn, waits=waits, s=s):
            for ss, v in waits:
                e.wait_ge(ss, v)
            fn(e).then_inc(s, 16)
        self.prog[q].append(emit)
        self._record(tok, reads, writes)
        self.ninst += 1
        return tok

    def final_wait(self, eng, keys):
        waits = self._deps(eng, keys, ())

        def emit(e, waits=waits):
            for s, v in waits:
                e.wait_ge(s, v)
        self.prog[eng].append(emit)

    def run(self, block):
        P = self.prog

        @block.tensor
        def _(e):
            for f in P["pe"]:
                f(e)

        @block.scalar
        def _(e):
            for f in P["act"]:
                f(e)

        @block.vector
        def _(e):
            for f in P["dve"]:
                f(e)

        @block.gpsimd
        def _(e):
            for f in P["pool"]:
                f(e)

        @block.sync
        def _(e):
            for f in P["sp"]:
                f(e)


V_G1, V_GM, V_G2, V_GF = 0, 8, 16, 24
V_BGLU, V_DSK = 32, 36
V_CQK = 40
V_CV = 72
V_LRE, V_LIM, V_LST = 88, 104, 120
V_ALOG, V_DTB, V_GON = 136, 144, 152
NV = 216
C_ID, C_TRI, C_NEG, C_STR, C_ONE, C_IP32 = 0, 128, 256, 384, 512, 640
NCON = 672


def build(nseq=4, stage=99):
    nc = bass.Bass("TRN2", target_bir_lowering=False)

    def din(name, shape):
        return nc.dram_tensor(name, list(shape), F32, kind="ExternalInput").ap()

    xT = din("xT", [nseq, 128, 8, SEQ])
    cT = din("cT", [128, 8, 4])
    wada = din("wada", [72, 128, 8 * 128])
    bada = din("bada", [128, 72])
    w1b_d = din("w1b", [22, 128, 8 * 128])
    w3b_d = din("w3b", [22, 128, 8 * 128])
    w2b_d = din("w2b", [8, 128, 22 * 128])
    w1 = [din("w1a", [22, 128, 8 * 128]), None]
    w3 = [din("w3a", [22, 128, 8 * 128]), None]
    w2 = [din("w2a", [8, 128, 22 * 128]), None]
    wq = din("wq", [8, 128, 8 * 64])
    wk = din("wk", [8, 128, 8 * 64])
    wv = din("wv", [4, 128, 8 * 128])
    wz = din("wz", [2, 128, 8 * 256])
    wba = din("wba", [1, 128, 8 * 16])
    ws5 = din("ws5", [4, 128, 8 * 128])
    wga = din("wga", [8, 128, 8 * 128])
    wgb = din("wgb", [8, 128, 8 * 128])
    wpa = din("wpa", [8, 128, 4 * 128])
    wpb = din("wpb", [8, 128, 4 * 128])
    wglu = din("wglu", [4, 128, 4 * 128])
    wout = din("wout", [8, 128, 8 * 128])
    vecs_d = din("vecs", [128, NV])
    cons_d = din("cons", [128, NCON])
    s5bc_d = din("s5bc", [128, 4 * 256])
    masks_d = din("masks", [128, 9 * 128])
    oT = nc.dram_tensor("oT", [nseq, 128, 8, SEQ], F32, kind="ExternalOutput").ap()
    tab_d = nc.dram_tensor("tabs", [16, 2, 128, SEQ], BF16, kind="Internal").ap()

    with ExitStack() as es:
        S = Sched(nc, es)

        def sb(name, shape, dt=F32):
            return es.enter_context(nc.sbuf_tensor(name, list(shape), dt))

        hT = sb("hT", [128, 8, 1024])
        arena = sb("arena", [128, NSLOT * SLOT])
        wsm = [sb("wsmall%d" % i, [128, 1024], BF16) for i in range(7)]
        wlg = [sb("wlarge%d" % i, [128, 2816], BF16) for i in range(3)]
        vecs = sb("vecs_s", [128, NV])
        cons = sb("cons_s", [128, NCON])
        consb = sb("consb", [128, 3 * 128], BF16)
        convqk = sb("convqk", [128, 8, 4, 64], BF16)
        convv = sb("convv", [128, 4, 4, 128], BF16)
        BbT = sb("BbT", [128, 4, 2, 128], BF16)
        CT = sb("CT", [128, 16, 2, 32], BF16)
        Dd = sb("Dd", [128, 4, 32], BF16)
        s5v = sb("s5v", [128, 256])
        modT = sb("modT", [128, 72, 4])
        GS = sb("GS", [128, 3, 8, 4])
        HG = sb("HG", [128, 3, 8, 4])
        sm = sb("sm", [128, 512])
        Sst = sb("Sst", [64, 512])
        Sbf = sb("Sbf", [64, 512], BF16)
        zst = sb("zst", [128, 16, 2])
        qkpre = sb("qkpre", [128, 8, 515], BF16)
        vpre = sb("vpre", [128, 4, 515], BF16)
        vnb = sb("vnb", [128, 512], BF16)
        masks = sb("masks_s", [128, 9 * 128], BF16)
        s5i = sb("s5i", [128, 16], mybir.dt.int32)
        psum = es.enter_context(nc.psum_tensor("ps", [128, 8 * 512], F32))
        block = es.enter_context(nc.Block())

        def region(s0, off_f32, n_f32, dt=F32, parts=128):
            a = s0 * SLOT + off_f32
            ap = arena[0:parts, a:a + n_f32]
            if dt == BF16:
                ap = ap.bitcast(BF16)
            keys = [("sq", i) for i in range(a // 256, (a + n_f32 - 1) // 256 + 1)]
            return ap, keys

        psn = [0]

        def getps():
            i = psn[0] % 8
            psn[0] += 1
            return psum[:, i * 512:(i + 1) * 512], [("ps", i)]

        ps2n = [0]

        def getps2():
            i = (ps2n[0] % 4) * 2
            ps2n[0] += 1
            return psum[:, i * 512:(i + 2) * 512], [("ps", i), ("ps", i + 1)]

        wsn = [0]
        wln = [0]

        def load_w(W, m, nel):
            ap, kcv = W
            if nel <= 1024:
                i = wsn[0] % len(wsm)
                wsn[0] += 1
                buf, key = wsm[i], ("ws", i)
            else:
                i = wln[0] % len(wlg)
                wln[0] += 1
                buf, key = wlg[i], ("wl", i)
            dst = buf[:, 0:nel]
            S.dma("sp", lambda e, dst=dst, ap=ap, m=m: e.dma_start(out=dst, in_=ap[m]), reads=kcv, writes=[key])
            return buf, [key]

        def conv_w(name, src):
            shp = list(src.shape)
            dst = nc.dram_tensor(name + "_bf", shp, BF16, kind="Internal").ap()
            S.dma("pool", lambda e: e.dma_start(out=dst.rearrange("m p n -> (m p) n"), in_=src.rearrange("m p n -> (m p) n"),
                                               max_dma_last_dim=4096), writes=[("wcv", name)])
            return dst, [("wcv", name)]

        def mm(out, lhsT, rhs, reads, writes, start=True, stop=True, tp=None):
            if tp is None:
                S.op("pe", lambda e: e.matmul(out, lhsT=lhsT, rhs=rhs, start=start, stop=stop), reads=reads, writes=writes)
            else:
                S.op("pe", lambda e: e.matmul(out, lhsT=lhsT, rhs=rhs, start=start, stop=stop, tile_position=tp),
                     reads=reads, writes=writes)

        def act(out, in_, func, reads, writes, bias=None, scale=None):
            kw = {}
            if bias is not None:
                kw["bias"] = bias
            if scale is not None:
                kw["scale"] = scale
            S.op("act", lambda e: e.activation(out=out, in_=in_, func=func, **kw), reads=reads, writes=writes)

        def tt(eng, out, in0, in1, op, reads, writes):
            S.op(eng, lambda e: e.tensor_tensor(out=out, in0=in0, in1=in1, op=op), reads=reads, writes=writes)

        def ts(eng, out, in0, s1, op0, reads, writes, s2=None, op1=None):
            if op1 is None:
                S.op(eng, lambda e: e.tensor_scalar(out=out, in0=in0, scalar1=s1, scalar2=None, op0=op0), reads=reads, writes=writes)
            else:
                S.op(eng, lambda e: e.tensor_scalar(out=out, in0=in0, scalar1=s1, scalar2=s2, op0=op0, op1=op1),
                     reads=reads, writes=writes)

        def stt(eng, out, in0, scalar, in1, op0, op1, reads, writes):
            S.op(eng, lambda e: e.scalar_tensor_tensor(out=out, in0=in0, scalar=scalar, in1=in1, op0=op0, op1=op1),
                 reads=reads, writes=writes)

        def cpy(eng, out, in_, reads, writes):
            if eng == "act":
                S.op("act", lambda e: e.copy(out=out, in_=in_), reads=reads, writes=writes)
            else:
                S.op(eng, lambda e: e.tensor_copy(out=out, in_=in_), reads=reads, writes=writes)

        ident = cons[:, C_ID:C_ID + 128]
        tri = cons[:, C_TRI:C_TRI + 128]
        negm = cons[:, C_NEG:C_NEG + 128]
        strict = cons[:, C_STR:C_STR + 128]
        onesf = cons[:, C_ONE:C_ONE + 128]
        ip32 = cons[:, C_IP32:C_IP32 + 32]
        identb = consb[:, 0:128]
        onesD = consb[:, 128:256]
        bones = consb[:, 256:384]
        KC = ["cons"]

        S.dma("sp", lambda e: e.dma_start(out=vecs[:], in_=vecs_d), writes=["vecs"])
        S.dma("sp", lambda e: e.dma_start(out=cons[:], in_=cons_d), writes=["cons"])
        S.dma("pool", lambda e: e.dma_start(out=masks[:], in_=masks_d), writes=["masks"])
        w1 = [conv_w("w1a", w1[0]), None]
        w3 = [conv_w("w3a", w3[0]), None]
        w2 = [conv_w("w2a", w2[0]), None]
        wq = conv_w("wq", wq)
        wk = conv_w("wk", wk)
        wv = conv_w("wv", wv)
        wz = conv_w("wz", wz)
        wba = conv_w("wba", wba)
        ws5 = conv_w("ws5", ws5)
        wglu = conv_w("wglu", wglu)
        wpa = conv_w("wpa", wpa)
        wpb = conv_w("wpb", wpb)
        wga = conv_w("wga", wga)
        wgb = conv_w("wgb", wgb)
        wout = conv_w("wout", wout)
        w1[1] = conv_w("w1b", w1b_d)
        w3[1] = conv_w("w3b", w3b_d)
        w2[1] = conv_w("w2b", w2b_d)
        epsc = s5v[:, 180:181]
        eps64 = s5v[:, 181:182]
        onec = s5v[:, 182:183]
        c64 = s5v[:, 183:184]
        S.op("dve", lambda e: e.memset(epsc, EPS), writes=["s5vc"])
        S.op("dve", lambda e: e.memset(eps64, 64.0 * EPS), writes=["s5vc"])
        S.op("dve", lambda e: e.memset(onec, 1.0), writes=["s5vc"])
        S.op("dve", lambda e: e.memset(c64, 64.0), writes=["s5vc"])
        cpy("dve", identb, ident, ["cons"], ["consb"])
        ts("dve", onesD, onesf, 1.0 / D, ALU.mult, ["cons"], ["consb"])
        S.op("dve", lambda e: e.memset(bones, 0.0), writes=["consb"])
        S.op("dve", lambda e: e.memset(consb[0:64, 256:320], 1.0), writes=["consb"])
        S.op("dve", lambda e: e.memset(consb[64:128, 320:384], 1.0), writes=["consb"])
        for h in range(8):
            for j in range(4):
                col = V_CQK + h * 4 + j
                ts("dve", convqk[0:64, h, j, :], ident[0:64, 0:64], vecs[0:64, col:col + 1], ALU.mult,
                   ["cons", "vecs"], ["convqk"])
                ts("dve", convqk[64:128, h, j, :], ident[64:128, 64:128], vecs[64:128, col:col + 1], ALU.mult,
                   ["cons", "vecs"], ["convqk"])
        for m in range(4):
            for j in range(4):
                col = V_CV + m * 4 + j
                ts("dve", convv[:, m, j, :], ident, vecs[:, col:col + 1], ALU.mult, ["cons", "vecs"], ["convv"])
        negexpA = s5v[:, 184:192]
        act(negexpA, vecs[:, V_ALOG:V_ALOG + 8], AF.Exp, ["vecs"], ["negexpA"])
        ts("dve", negexpA, negexpA, -1.0, ALU.mult, ["negexpA"], ["negexpA"])

        csil, kcs = region(0, 0, 32)
        S.dma("sp", lambda e: e.dma_start(out=csil, in_=cT.rearrange("p k b -> p (k b)")), writes=kcs)
        act(csil, csil, AF.Silu, kcs, kcs)
        badas, kba = region(0, 64, 72)
        S.dma("sp", lambda e: e.dma_start(out=badas, in_=bada), writes=kba)
        psm, kpm = getps()
        for m in range(72):
            wt, kw = region(1 + (m % 8), 0, 1024)
            S.dma("sp", lambda e, wt=wt, m=m: e.dma_start(out=wt, in_=wada[m]), writes=kw)
            for k in range(8):
                mm(psm[:, m * 4:m * 4 + 4], wt[:, k * 128:(k + 1) * 128], csil[:, k * 4:k * 4 + 4],
                   kw + kcs, kpm, start=(k == 0), stop=(k == 7))
        mod3 = modT[:]
        tt("dve", mod3, psm[:, 0:288].rearrange("p (m b) -> p m b", b=4),
           badas.unsqueeze(2).broadcast_to([128, 72, 4]), ALU.add, kpm + kba, ["modT"])
        for i, (gcol, hgs) in enumerate(((V_G1, 0.5), (V_GM, 1.0), (V_G2, 0.5))):
            sc = modT[:, (3 * i + 1) * 8:(3 * i + 2) * 8, :]
            gt = modT[:, (3 * i + 2) * 8:(3 * i + 3) * 8, :]
            ts("dve", GS[:, i], sc, 1.0, ALU.add, ["modT"], ["GS"])
            tt("dve", GS[:, i], GS[:, i], vecs[:, gcol:gcol + 8].unsqueeze(2).broadcast_to([128, 8, 4]), ALU.mult,
               ["GS", "vecs"], ["GS"])
            ts("dve", HG[:, i], gt, hgs, ALU.mult, ["modT"], ["HG"])

        s5bc, ks5 = region(10, 0, 1024)
        S.dma("sp", lambda e: e.dma_start(out=s5bc, in_=s5bc_d), writes=ks5)
        lre = vecs[:, V_LRE:V_LRE + 16]
        lim = vecs[:, V_LIM:V_LIM + 16]

        def sv(i):
            return s5v[:, i * 16:(i + 1) * 16]
        stp, lrc, mag, ang, lbr, lbi, den, cre_, cim_, t1, t2 = [sv(i) for i in range(11)]
        K5 = ["s5v"]
        act(stp, vecs[:, V_LST:V_LST + 16], AF.Exp, ["vecs"], K5)
        ts("dve", lrc, lre, -1e-4, ALU.min, ["vecs"], K5)
        tt("dve", t1, lrc, stp, ALU.mult, K5, K5)
        act(mag, t1, AF.Exp, K5, K5)
        tt("dve", ang, lim, stp, ALU.mult, K5 + ["vecs"], K5)
        PI = float(np.pi)
        C1 = 6.28125
        C2 = 2 * PI - C1
        for (dst, shift, tmp) in ((lbi, 0.0, t1), (lbr, 0.5 * PI, t2)):
            ts("dve", tmp, ang, shift, ALU.add, K5, K5)
            ts("dve", den, tmp, 1.0 / (2 * PI), ALU.mult, K5, K5)
            cpy("dve", s5i[:, 0:16], den, K5, ["s5i"])
            cpy("dve", den, s5i[:, 0:16], ["s5i"], K5)
            stt("dve", tmp, den, -C1, tmp, ALU.mult, ALU.add, K5, K5)
            stt("dve", tmp, den, -C2, tmp, ALU.mult, ALU.add, K5, K5)
            ts("dve", tmp, tmp, PI, ALU.min, K5, K5, s2=-PI, op1=ALU.max)
            act(dst, tmp, AF.Sin, K5, K5)
        tabre, ktr = region(20, 0, 2048)
        tabim, kti = region(22, 0, 2048)
        ttmp, ktt = region(24, 0, 1024)
        ttmp2, ktt2 = region(25, 0, 1024)
        seedr = s5v[:, 192:208]
        seedi = s5v[:, 208:224]
        cpy("dve", seedr, lbr, K5, ["seed"])
        cpy("dve", seedi, lbi, K5, ["seed"])
        def table_gen():
            for pt in range(16):
                if stage < 3:
                    break
                cpy("dve", tabre[:, 0:1], seedr[:, pt:pt + 1], ["seed"], ktr)
                cpy("dve", tabim[:, 0:1], seedi[:, pt:pt + 1], ["seed"], kti)
                mlen = 1
                while mlen < SEQ:
                    cr = tabre[:, mlen - 1:mlen]
                    ci = tabim[:, mlen - 1:mlen]
                    ts("dve", ttmp[:, 0:mlen], tabim[:, 0:mlen], ci, ALU.mult, kti, ktt)
                    ts("dve", ttmp2[:, 0:mlen], tabim[:, 0:mlen], cr, ALU.mult, kti + ktr, ktt2)
                    stt("dve", tabre[:, mlen:2 * mlen], tabre[:, 0:mlen], cr, ttmp[:, 0:mlen], ALU.mult, ALU.subtract,
                        ktr + ktt, ktr)
                    stt("dve", tabim[:, mlen:2 * mlen], tabre[:, 0:mlen], ci, ttmp2[:, 0:mlen], ALU.mult, ALU.add,
                        ktr + kti + ktt2, kti)
                    mlen *= 2
                S.dma("pool", lambda e, pt=pt: e.dma_start(out=tab_d[pt, 0], in_=tabre, max_dma_last_dim=4096), reads=ktr, writes=[("tab", pt)])
                S.dma("pool", lambda e, pt=pt: e.dma_start(out=tab_d[pt, 1], in_=tabim, max_dma_last_dim=4096), reads=kti, writes=[("tab", pt)])
                yield
        bg_tables = table_gen()
        tt("dve", lbr, lbr, mag, ALU.mult, K5, K5)
        tt("dve", lbi, lbi, mag, ALU.mult, K5, K5)
        tt("dve", den, lrc, lrc, ALU.mult, K5, K5)
        tt("dve", t1, lim, lim, ALU.mult, ["vecs"] + K5, K5)
        tt("dve", den, den, t1, ALU.add, K5, K5)
        S.op("dve", lambda e: e.reciprocal(out=den, in_=den), reads=K5, writes=K5)
        ts("dve", t2, lbr, -1.0, ALU.add, K5, K5)
        tt("dve", cre_, t2, lrc, ALU.mult, K5, K5)
        tt("dve", t1, lbi, lim, ALU.mult, K5 + ["vecs"], K5)
        tt("dve", cre_, cre_, t1, ALU.add, K5, K5)
        tt("dve", cre_, cre_, den, ALU.mult, K5, K5)
        tt("dve", cim_, lbi, lrc, ALU.mult, K5, K5)
        tt("dve", t1, t2, lim, ALU.mult, K5 + ["vecs"], K5)
        tt("dve", cim_, cim_, t1, ALU.subtract, K5, K5)
        tt("dve", cim_, cim_, den, ALU.mult, K5, K5)
        b_re = s5bc[:, 0:256].rearrange("p (t c) -> p t c", c=16)
        b_im = s5bc[:, 256:512].rearrange("p (t c) -> p t c", c=16)
        c_re = s5bc[:, 512:768].rearrange("p (t c) -> p t c", c=16)
        c_im = s5bc[:, 768:1024].rearrange("p (t c) -> p t c", c=16)
        bb, kbb = region(11, 0, 1024)
        bbr = bb[:, 0:256].rearrange("p (t c) -> p t c", c=16)
        bbi = bb[:, 256:512].rearrange("p (t c) -> p t c", c=16)
        btm = bb[:, 512:768].rearrange("p (t c) -> p t c", c=16)
        creb = cre_.unsqueeze(2).broadcast_to([128, 16, 16])
        cimb = cim_.unsqueeze(2).broadcast_to([128, 16, 16])
        tt("dve", bbr, b_re, creb, ALU.mult, ks5 + K5, kbb)
        tt("dve", btm, b_im, cimb, ALU.mult, ks5 + K5, kbb)
        tt("dve", bbr, bbr, btm, ALU.subtract, kbb, kbb)
        tt("dve", bbi, b_im, creb, ALU.mult, ks5 + K5, kbb)
        tt("dve", btm, b_re, cimb, ALU.mult, ks5 + K5, kbb)
        tt("dve", bbi, bbi, btm, ALU.add, kbb, kbb)
        blk, kbl = region(18, 0, 512, BF16)
        blkr = blk[:, 0:512].rearrange("p (t c) -> p t c", c=32)
        blki = blk[:, 512:1024].rearrange("p (t c) -> p t c", c=32)
        S.op("dve", lambda e: e.memset(blk, 0.0), writes=kbl)
        for (dst, src) in ((blkr, bbr), (blki, bbi)):
            cpy("dve", dst[0:64, :, 0:16], src[0:64], kbb, kbl)
            cpy("dve", dst[64:128, :, 16:32], src[64:128], kbb, kbl)
        for m in range(4):
            for ri, src in enumerate((blkr, blki)):
                pst, kps = getps()
                for q in range(4):
                    mm(pst[32 * q:32 * q + 32, 0:128], src[:, 4 * m + q, :], identb, kbl + ["consb"], kps, tp=(0, 32 * q))
                cpy("act", BbT[:, m, ri, :], pst[:, 0:128], kps, ["BbT"])
        S.op("dve", lambda e: e.memset(CT[:], 0.0), writes=["CT"])
        cpy("dve", CT[0:64, :, 0, 0:16], c_re[0:64], ks5, ["CT"])
        cpy("dve", CT[64:128, :, 0, 16:32], c_re[64:128], ks5, ["CT"])
        ts("dve", CT[0:64, :, 1, 0:16], c_im[0:64], -1.0, ALU.mult, ks5, ["CT"])
        ts("dve", CT[64:128, :, 1, 16:32], c_im[64:128], -1.0, ALU.mult, ks5, ["CT"])
        for m in range(4):
            ts("dve", Dd[:, m, :], ip32, vecs[:, V_DSK + m:V_DSK + m + 1], ALU.mult, ["cons", "vecs"], ["Dd"])
        rmag = mag

        def norm_mod(b, t0, n, ni, uT, ku, s_tmp):
            for s in range(n // 512):
                tsl = slice(t0 + s * 512, t0 + (s + 1) * 512)
                pss, kp = getps()
                for k in range(8):
                    sq, ksq = region(s_tmp, (k % 2) * 256, 256, BF16)
                    act(sq, hT[:, k, tsl], AF.Square, [("h", k, tsl.start // 512)], ksq)
                    mm(pss, onesD, sq, ksq + ["consb"], kp, start=(k == 0), stop=(k == 7))
                rstd, krs = region(s_tmp + 1, 0, 512)
                act(rstd, pss, AF.Ln, kp + ["s5vc"], krs, bias=epsc, scale=1.0)
                act(rstd, rstd, AF.Exp, krs, krs, scale=-0.5)
                for k in range(8):
                    tm, ktm = region(s_tmp + 2, (k % 2) * 512, 512)
                    tt("dve", tm, hT[:, k, tsl], rstd, ALU.mult, [("h", k, tsl.start // 512)] + krs, ktm)
                    act(uT[:, k, s * 512:(s + 1) * 512], tm, AF.Identity, ktm + ["GS", "modT"], ku,
                        bias=modT[:, (3 * ni) * 8 + k, b:b + 1], scale=GS[:, ni, k, b:b + 1])

        def ffn(b, t0, fi, ni):
            NT = 1024
            u, ku = region(0, 0, 4096, BF16)
            uT = u.rearrange("p (k t) -> p k t", t=NT)
            g, kg = region(4, 0, 11264, BF16)
            gT = g.rearrange("p (k t) -> p k t", t=NT)
            norm_mod(b, t0, NT, ni, uT, ku, 15)
            for m in range(22):
                wa, kwa = load_w(w1[fi], m, 1024)
                wb_, kwb = load_w(w3[fi], m, 1024)
                for s in range(2):
                    p1, k1 = getps()
                    p3, k3 = getps()
                    for k in range(8):
                        mm(p1, wa[:, k * 128:(k + 1) * 128], uT[:, k, s * 512:(s + 1) * 512], kwa + ku, k1,
                           start=(k == 0), stop=(k == 7))
                    for k in range(8):
                        mm(p3, wb_[:, k * 128:(k + 1) * 128], uT[:, k, s * 512:(s + 1) * 512], kwb + ku, k3,
                           start=(k == 0), stop=(k == 7))
                    sl_, ksl = region(18, ((2 * m + s) % 2) * 512, 512)
                    act(sl_, p1, AF.Silu, k1, ksl)
                    tt("dve", gT[:, m, s * 512:(s + 1) * 512], sl_, p3, ALU.mult, ksl + k3, kg)
                next(bg_tables, None)
            for m in range(8):
                wc, kwc = load_w(w2[fi], m, 2816)
                for s in range(2):
                    p, kp = getps()
                    for k in range(22):
                        mm(p, wc[:, k * 128:(k + 1) * 128], gT[:, k, s * 512:(s + 1) * 512], kwc + kg, kp,
                           start=(k == 0), stop=(k == 21))
                    hs = hT[:, m, t0 + s * 512:t0 + (s + 1) * 512]
                    stt("dve", hs, p, HG[:, ni, m, b:b + 1], hs, ALU.mult, ALU.add, kp + ["HG", ("h", m, s)], [("h", m, s)])

        def mixer(b, tt_i):
            t0 = (tt_i % 2) * 512
            tg0 = tt_i * 512
            NT = 512
            um, kum = region(0, 0, 2048, BF16)
            umT = um.rearrange("p (k t) -> p k t", t=NT)
            norm_mod(b, t0, NT, 1, umT, kum, 17)
            qTa, kqT = region(2, 0, 2048, BF16)
            kTa, kkT = region(4, 0, 2048, BF16)
            qT = qTa.rearrange("p (h t) -> p h t", t=NT)
            kT = kTa.rearrange("p (h t) -> p h t", t=NT)
            vta, kvt = region(6, 0, 1024, BF16)
            kta, kkt = region(7, 0, 1024, BF16)
            zsa, kzs = region(8, 0, 1024, BF16)
            vtok = vta.rearrange("p (c f) -> p c f", f=512)
            ktok = kta.rearrange("p (c f) -> p c f", f=512)
            zs = zsa.rearrange("p (c f) -> p c f", f=512)
            oga, kog = region(19, 0, 1024, BF16)
            ogT = oga.rearrange("p (m t) -> p m t", t=NT)
            KQ = [("qkpre", h_) for h_ in range(8)]
            KV = [("vpre", m_) for m_ in range(4)]
            if stage >= 2:
                if tt_i == 0:
                    S.op("pool", lambda e: e.memset(qkpre[:, :, 0:3], 0.0), writes=KQ)
                    S.op("pool", lambda e: e.memset(vpre[:, :, 0:3], 0.0), writes=KV)
                    S.op("pool", lambda e: e.memset(Sst[:], 0.0), writes=[("S", 0), ("S", 1)])
                    S.op("pool", lambda e: e.memset(Sbf[:], 0.0), writes=[("Sbf", 0), ("Sbf", 1)])
                else:
                    cpy("pool", qkpre[:, :, 0:3], qkpre[:, :, 512:515], KQ, KQ)
                    cpy("pool", vpre[:, :, 0:3], vpre[:, :, 512:515], KV, KV)
                for h in range(8):
                    wa, kwa = load_w(wq, h, 512)
                    wb_, kwb = load_w(wk, h, 512)
                    p, kp = getps()
                    for k in range(8):
                        mm(p[0:64, :], wa[:, k * 64:(k + 1) * 64], umT[:, k, :], kwa + kum, kp, start=(k == 0), stop=(k == 7))
                    for k in range(8):
                        mm(p[64:128, :], wb_[:, k * 64:(k + 1) * 64], umT[:, k, :], kwb + kum, kp, start=(k == 0),
                           stop=(k == 7), tp=(0, 64))
                    cpy("act", qkpre[:, h, 3:515], p, kp, [("qkpre", h)])
                for m in range(4):
                    wa, kwa = load_w(wv, m, 1024)
                    p, kp = getps()
                    for k in range(8):
                        mm(p, wa[:, k * 128:(k + 1) * 128], umT[:, k, :], kwa + kum, kp, start=(k == 0), stop=(k == 7))
                    cpy("act", vpre[:, m, 3:515], p, kp, [("vpre", m)])
                for h in range(8):
                    for which in range(2):
                        base = 64 * which
                        p, kp = getps()
                        for j in range(4):
                            mm(p[0:64, :], convqk[base:base + 64, h, j, :], qkpre[base:base + 64, h, j:j + 512],
                               [("qkpre", h), "convqk"], kp, start=(j == 0), stop=(j == 3), tp=(base, 0))
                        dstT = (qT if which == 0 else kT)
                        act(dstT[0:64, h, :], p[0:64, :], AF.Silu, kp, (kqT if which == 0 else kkT)[h:h + 1])
                for m in range(4):
                    p, kp = getps()
                    for j in range(4):
                        mm(p, convv[:, m, j, :], vpre[:, m, j:j + 512], [("vpre", m), "convv"], kp, start=(j == 0), stop=(j == 3))
                    vs, kvs = region(15 + (m % 2), 0, 256, BF16)
                    act(vs, p, AF.Silu, kp, kvs)
                    pt_, kpt = getps()
                    ptb = pt_.bitcast(BF16)
                    for c in range(4):
                        S.op("pe", lambda e, c=c, ptb=ptb, vs=vs: e.transpose(out=ptb[:, c * 128:(c + 1) * 128],
                                                                              in_=vs[:, c * 128:(c + 1) * 128], identity=identb),
                             reads=kvs + ["consb"], writes=kpt)
                    cpy("act", vtok[:, :, m * 128:(m + 1) * 128], ptb[:, 0:512].rearrange("p (c f) -> p c f", f=128), kpt, kvt)
                wz0, kz0 = load_w(wz, 0, 2048)
                wz1, kz1 = load_w(wz, 1, 2048)
                for c in range(4):
                    p, kp = getps()
                    for hf, (wzz, kzz) in enumerate(((wz0, kz0), (wz1, kz1))):
                        for k in range(8):
                            mm(p[:, hf * 256:(hf + 1) * 256], umT[:, k, c * 128:(c + 1) * 128], wzz[:, k * 256:(k + 1) * 256],
                               kum + kzz, kp, start=(k == 0), stop=(k == 7))
                    act(zs[:, c, :], p, AF.Silu, kp, kzs)
                for h in range(8):
                    for which in range(2):
                        dstT = (qT if which == 0 else kT)
                        kd = (kqT if which == 0 else kkT)[h:h + 1]
                        st_ = 9 + 2 * ((2 * h + which) % 3)
                        sq, ksq = region(st_ + 1, 512, 256, BF16, 64)
                        act(sq, dstT[0:64, h, :], AF.Square, kd, ksq)
                        p2, kp2 = getps()
                        mm(p2[0:64, :], bones[0:64, 0:64], sq, ksq + ["consb"], kp2)
                        rs, krs = region(st_ + 1, 0, 512, F32, 64)
                        if which == 0:
                            act(rs, p2[0:64, :], AF.Ln, kp2 + ["s5vc"], krs, bias=eps64[0:64], scale=64.0)
                        else:
                            act(rs, p2[0:64, :], AF.Ln, kp2 + ["s5vc"], krs, bias=epsc[0:64], scale=1.0)
                        act(rs, rs, AF.Exp, krs, krs, scale=-0.5)
                        tt("dve", dstT[0:64, h, :], dstT[0:64, h, :], rs, ALU.mult, kd + krs, kd)
                for hh in range(2):
                    pt_, kpt = getps()
                    ptb = pt_.bitcast(BF16)
                    for h4 in range(4):
                        h = hh * 4 + h4
                        for c in range(4):
                            o_ = ptb[:, (h4 * 4 + c) * 64:(h4 * 4 + c + 1) * 64]
                            S.op("pe", lambda e, o_=o_, h=h, c=c: e.transpose(out=o_, in_=kT[0:64, h, c * 128:(c + 1) * 128],
                                                                              identity=identb[0:64, 0:64]),
                                 reads=kkT + ["consb"], writes=kpt)
                    cpy("act", ktok[:, :, hh * 256:(hh + 1) * 256].rearrange("p c (h d) -> p h c d", d=64),
                        ptb.rearrange("p (h c d) -> p h c d", c=4, d=64), kpt, kkt)
                wb_, kwb = load_w(wba, 0, 128)
                pb, kpb = getps()
                for c in range(4):
                    for k in range(8):
                        mm(pb[:, c * 16:(c + 1) * 16], umT[:, k, c * 128:(c + 1) * 128], wb_[:, k * 16:(k + 1) * 16],
                           kum + kwb, kpb, start=(k == 0), stop=(k == 7))
                pb3 = pb[:, 0:64].rearrange("p (c f) -> p c f", f=16)
                beta = sm[:, 0:32].rearrange("p (c h) -> p c h", h=8)
                nbeta = sm[:, 32:64].rearrange("p (c h) -> p c h", h=8)
                la = sm[:, 64:96].rearrange("p (c h) -> p c h", h=8)
                gcum = sm[:, 96:128].rearrange("p (c h) -> p c h", h=8)
                KS = ["sm"]
                act(beta, pb3[:, :, 0:8], AF.Exp, kpb, KS, scale=-1.0)
                ts("dve", beta, beta, 1.0, ALU.add, KS, KS)
                S.op("dve", lambda e, beta=beta: e.reciprocal(out=beta, in_=beta), reads=KS, writes=KS)
                ts("dve", nbeta, beta, -1.0, ALU.mult, KS, KS)
                tt("dve", la, pb3[:, :, 8:16], vecs[:, V_DTB:V_DTB + 8].unsqueeze(1).broadcast_to([128, 4, 8]), ALU.add,
                   kpb + ["vecs"], KS)
                act(la, la, AF.Exp, KS, KS)
                act(la, la, AF.Ln, KS + ["s5vc"], KS, bias=onec, scale=1.0)
                tt("dve", la, la, negexpA.unsqueeze(1).broadcast_to([128, 4, 8]), ALU.mult, KS + ["negexpA"], KS)
                pg, kpg = getps()
                mm(pg[:, 0:32], tri, sm[:, 64:96], KC + KS, kpg)
                cpy("dve", sm[:, 96:128], pg[:, 0:32], kpg, KS)

                def chunk_gen(c, g):
                    csl = slice(c * 128, (c + 1) * 128)
                    hs = slice(4 * g, 4 * g + 4)
                    A_dg, k_dg = region(9, g * 512, 512)
                    A_dt, k_dt = region(10, g * 512, 512)
                    A_eg, k_eg = region(11, g * 512, 512)
                    A_m2, k_m2 = region(12, g * 512, 512)
                    A_at, k_at = region(13, g * 256, 256, BF16)
                    A_ek, k_ek = region(13, 512 + g * 256, 256, BF16, 64)
                    A_qd, k_qd = region(14, g * 256, 256, BF16, 64)
                    A_kd, k_kd = region(14, 512 + g * 128, 128, BF16)
                    A_R, k_R = region(14, 768 + g * 128, 128, BF16)
                    A_vn, k_vn = vnb[:, g * 256:(g + 1) * 256], [("vnb", g)]
                    A_P = [region(15, i_ * 512 + g * 256, 256, BF16) for i_ in range(2)]
                    A_Q = [region(16, i_ * 512 + g * 256, 256, BF16) for i_ in range(2)]
                    A_T = [region(17, i_ * 512 + g * 256, 256, BF16) for i_ in range(2)]
                    A_of, k_of = region(18, g * 256, 256)
                    A_o2, k_o2 = region(18, 512 + g * 256, 256)
                    kkTg = kkT[4 * g:4 * g + 4]
                    kqTg = kqT[4 * g:4 * g + 4]
                    KSg = KS
                    kSt, kSb = [("S", g)], [("Sbf", g)]
                    Sst_g = Sst[:, g * 256:(g + 1) * 256]
                    Sbf_g = Sbf[:, g * 256:(g + 1) * 256]

                    def v3(ap):
                        return ap.rearrange("p (h i) -> p h i", h=4)

                    def b4(ap2d):
                        return ap2d.unsqueeze(1).broadcast_to([128, 4, 128])
                    gc = gcum[:, c, hs]
                    gcb = gc.unsqueeze(2).broadcast_to([128, 4, 128])
                    tt("dve", v3(A_dg), b4(ident), gcb, ALU.mult, KC + KSg, k_dg)
                    pG, kG = getps()
                    mm(pG, onesf, A_dg, KC + k_dg, kG)
                    yield
                    tt("dve", v3(A_dt), v3(pG), gcb, ALU.subtract, kG + KSg, k_dt)
                    stt("dve", v3(A_dt), v3(A_dt), 0.0, b4(negm), ALU.min, ALU.add, k_dt + KC, k_dt)
                    act(A_dt, A_dt, AF.Exp, k_dt, k_dt)
                    act(A_eg, pG, AF.Exp, kG, k_eg)
                    edl = sm[:, 128 + 4 * g:132 + 4 * g]
                    kedl = [("edl", g)]
                    tt("dve", edl, v3(pG)[:, :, 127], gc, ALU.subtract, kG + KSg, kedl)
                    act(edl, edl, AF.Exp, kedl, kedl)
                    pK, kK = getps()
                    pQ, kQ_ = getps()
                    for hh in range(4):
                        h = 4 * g + hh
                        mm(pK[:, hh * 128:(hh + 1) * 128], kT[0:64, h, csl], kT[0:64, h, csl], kkTg, kK)
                    for hh in range(4):
                        h = 4 * g + hh
                        mm(pQ[:, hh * 128:(hh + 1) * 128], kT[0:64, h, csl], qT[0:64, h, csl], kkTg + kqTg, kQ_)
                    yield
                    tt("dve", A_at, pQ, A_dt, ALU.mult, kQ_ + k_dt, k_at)
                    tt("dve", v3(A_m2), b4(strict), nbeta[:, c, hs].unsqueeze(2).broadcast_to([128, 4, 128]), ALU.mult,
                       KC + KSg, k_m2)
                    tt("dve", A_dg, pK, A_dt, ALU.mult, kK + k_dt, k_dg)
                    P0, kP0 = A_P[0]
                    tt("dve", P0, A_dg, A_m2, ALU.mult, k_dg + k_m2, kP0)
                    pT_, kT_ = getps()
                    pTb = pT_.bitcast(BF16)
                    for hh in range(4):
                        S.op("pe", lambda e, hh=hh, pTb=pTb, P0=P0: e.transpose(out=pTb[:, hh * 128:(hh + 1) * 128],
                                                                                in_=P0[:, hh * 128:(hh + 1) * 128], identity=identb),
                             reads=kP0 + ["consb"], writes=kT_)
                    Q0, kQ0 = A_Q[0]
                    cpy("act", Q0, pTb[:, 0:512], kT_, kQ0)
                    yield
                    M_, kM = A_P[0]
                    MT_, kMT = A_Q[0]
                    B1, kB1 = region(9, g * 512, 256, BF16)
                    B2, kB2 = region(9, g * 512 + 256, 256, BF16)
                    B3, kB3 = region(12, g * 512, 256, BF16)
                    B4, kB4 = region(12, g * 512 + 256, 256, BF16)
                    Md, kMd = A_P[1]
                    MdT, kMdT = A_Q[1]

                    def mk(i_):
                        return b4(masks[:, i_ * 128:(i_ + 1) * 128])

                    def grp(lh, rh, rk):
                        pz, kz = getps()
                        for hh in range(4):
                            hs_ = slice(hh * 128, (hh + 1) * 128)
                            mm(pz[:, hs_], lh[:, hs_], rh[:, hs_], rk, kz)
                        return pz, kz
                    tt("dve", v3(Md), v3(M_), mk(0), ALU.mult, kM + ["masks"], kMd)
                    tt("dve", v3(MdT), v3(MT_), mk(0), ALU.mult, kMT + ["masks"], kMdT)
                    pz, kz = grp(MdT, Md, kMd + kMdT)
                    cpy("act", B1, pz, kz, kB1)
                    pz, kz = grp(Md, MdT, kMd + kMdT)
                    cpy("act", B2, pz, kz, kB2)
                    Ta, kTa = A_T[0]
                    TTa, kTTa = A_T[1]
                    idb = b4(identb)
                    tt("dve", v3(Ta), v3(Md), idb, ALU.add, kMd + ["consb"], kTa)
                    tt("dve", v3(TTa), v3(MdT), idb, ALU.add, kMdT + ["consb"], kTTa)
                    yield
                    pz, kz = grp(B2, B1, kB1 + kB2)
                    cpy("act", B3, pz, kz, kB3)
                    pz, kz = grp(B1, B2, kB1 + kB2)
                    cpy("act", B4, pz, kz, kB4)
                    pz, kz = grp(B2, Ta, kB2 + kTa)
                    tt("dve", Md, Ta, pz, ALU.add, kTa + kz, kMd)
                    pz, kz = grp(B1, TTa, kB1 + kTTa)
                    tt("dve", MdT, TTa, pz, ALU.add, kTTa + kz, kMdT)
                    yield
                    pz, kz = grp(B4, Md, kB4 + kMd)
                    tt("dve", Ta, Md, pz, ALU.add, kMd + kz, kTa)
                    pz, kz = grp(B3, MdT, kB3 + kMdT)
                    tt("dve", TTa, MdT, pz, ALU.add, kMdT + kz, kTTa)
                    yield
                    Tc, kTc, TTc, kTTc = Ta, kTa, TTa, kTTa
                    Tn, kTn, TTn, kTTn = Md, kMd, MdT, kMdT
                    for li in range(4):
                        last = (li == 3)
                        pz, kz = grp(MT_, Tc, kMT + kTc)
                        cpy("act", B1, pz, kz, kB1)
                        if not last:
                            pz2, kz2 = grp(M_, TTc, kM + kTTc)
                            cpy("act", B2, pz2, kz2, kB2)
                        pz, kz = grp(TTc, B1, kTTc + kB1)
                        tt("dve", v3(B3), v3(pz), mk(1 + 2 * li), ALU.mult, kz + ["masks"], kB3)
                        tt("dve", Tn, Tc, B3, ALU.add, kTc + kB3, kTn)
                        if not last:
                            pz2, kz2 = grp(Tc, B2, kTc + kB2)
                            tt("dve", v3(B4), v3(pz2), mk(2 + 2 * li), ALU.mult, kz2 + ["masks"], kB4)
                            tt("dve", TTn, TTc, B4, ALU.add, kTTc + kB4, kTTn)
                        Tc, kTc, Tn, kTn = Tn, kTn, Tc, kTc
                        TTc, kTTc, TTn, kTTn = TTn, kTTn, TTc, kTTc
                        yield
                    Tt, kTt = Tc, kTc
                    tt("dve", A_ek, kT[0:64, hs, csl], v3(A_eg)[0:64], ALU.mult, kkTg + k_eg, k_ek)
                    tt("dve", A_qd, qT[0:64, hs, csl], v3(A_eg)[0:64], ALU.mult, kqTg + k_eg, k_qd)
                    tt("dve", A_kd.rearrange("p (h d) -> p h d", d=64),
                       ktok[:, c, g * 256:(g + 1) * 256].rearrange("p (h d) -> p h d", d=64),
                       edl.unsqueeze(2).broadcast_to([128, 4, 64]), ALU.mult, kkt + kedl, k_kd)
                    ek3 = v3(A_ek)
                    qd3 = v3(A_qd)
                    pY, kY = getps()
                    for hh in range(4):
                        mm(pY[:, hh * 64:(hh + 1) * 64], ek3[:, hh, :], Sbf_g[:, hh * 64:(hh + 1) * 64], k_ek + kSb, kY)
                    tt("dve", A_R, vtok[:, c, g * 256:(g + 1) * 256], pY[:, 0:256], ALU.subtract, kvt + kY, k_R)
                    pX, kX = getps()
                    for hh in range(4):
                        mm(pX[:, hh * 64:(hh + 1) * 64], Tt[:, hh * 128:(hh + 1) * 128], A_R[:, hh * 64:(hh + 1) * 64], kTt + k_R, kX)
                    yield
                    tt("dve", A_vn.rearrange("p (h d) -> p h d", d=64), pX[:, 0:256].rearrange("p (h d) -> p h d", d=64),
                       beta[:, c, hs].unsqueeze(2).broadcast_to([128, 4, 64]), ALU.mult, kX + KSg, k_vn)
                    pO, kO = getps()
                    for hh in range(4):
                        mm(pO[:, hh * 64:(hh + 1) * 64], qd3[:, hh, :], Sbf_g[:, hh * 64:(hh + 1) * 64], k_qd + kSb, kO,
                           start=True, stop=False)
                        mm(pO[:, hh * 64:(hh + 1) * 64], A_at[:, hh * 128:(hh + 1) * 128], A_vn[:, hh * 64:(hh + 1) * 64],
                           k_at + k_vn, kO, start=False, stop=True)
                    pS, kS_ = getps()
                    for hh in range(4):
                        mm(pS[0:64, hh * 64:(hh + 1) * 64], A_kd[:, hh * 64:(hh + 1) * 64], A_vn[:, hh * 64:(hh + 1) * 64],
                           k_kd + k_vn, kS_)
                    yield
                    S3 = Sst_g.rearrange("p (h d) -> p h d", d=64)
                    tt("dve", S3, S3, v3(A_eg)[0:64, :, 127:128].broadcast_to([64, 4, 64]), ALU.mult, kSt + k_eg, kSt)
                    tt("dve", Sst_g, Sst_g, pS[0:64, 0:256], ALU.add, kSt + kS_, kSt)
                    cpy("act", Sbf_g, Sst_g, kSt, kSb)
                    cpy("act", A_of, pO[:, 0:256], kO, k_of)
                    yield
                    tt("dve", A_o2, A_of, A_of, ALU.mult, k_of, k_o2)
                    ss8 = sm[:, 136 + 4 * g:140 + 4 * g]
                    kss = [("ss8", g)]
                    S.op("dve", lambda e, A_o2=A_o2, ss8=ss8: e.tensor_reduce(out=ss8, in_=A_o2.rearrange("p (h d) -> p h d", d=64),
                                                                              axis=AX.X, op=ALU.add), reads=k_o2, writes=kss)
                    act(ss8, ss8, AF.Ln, kss + ["s5vc"], kss, bias=epsc, scale=1.0 / 64)
                    act(ss8, ss8, AF.Exp, kss, kss, scale=-0.5)
                    tt("dve", A_o2.rearrange("p (h d) -> p h d", d=64),
                       zs[:, c, g * 256:(g + 1) * 256].rearrange("p (h d) -> p h d", d=64),
                       vecs[:, V_GON:V_GON + 64].unsqueeze(1).broadcast_to([128, 4, 64]), ALU.mult, kzs + ["vecs"], k_o2)
                    yield
                    tt("dve", A_of.rearrange("p (h d) -> p h d", d=64), A_of.rearrange("p (h d) -> p h d", d=64),
                       ss8.unsqueeze(2).broadcast_to([128, 4, 64]), ALU.mult, k_of + kss, k_of)
                    og, kogc = region(13, g * 256, 128, BF16)
                    tt("dve", og, A_of, A_o2, ALU.mult, k_of + k_o2, kogc)
                    pT2, kT2 = getps()
                    pT2b = pT2.bitcast(BF16)
                    for mm_ in range(2):
                        S.op("pe", lambda e, mm_=mm_, pT2b=pT2b, og=og: e.transpose(out=pT2b[:, mm_ * 128:(mm_ + 1) * 128],
                                                                                    in_=og[:, mm_ * 128:(mm_ + 1) * 128], identity=identb),
                             reads=kogc + ["consb"], writes=kT2)
                    cpy("act", ogT[:, 2 * g:2 * g + 2, csl], pT2b[:, 0:256].rearrange("p (m t) -> p m t", t=128), kT2,
                        kog[2 * g:2 * g + 2])
                    yield

                for c in range(4):
                    gens = [chunk_gen(c, 0), chunk_gen(c, 1)]
                    alive = [True, True]
                    while any(alive):
                        for gi in range(2):
                            if alive[gi]:
                                try:
                                    next(gens[gi])
                                except StopIteration:
                                    alive[gi] = False
            else:
                S.op("pool", lambda e: e.memset(oga, 0.0), writes=kog)

            ybha, kyb = region(8, 0, 1024, BF16)
            ybh = ybha.rearrange("p (m t) -> p m t", t=NT)
            if stage >= 3:
                s5ua, ks5u = region(9, 0, 1024, BF16)
                s5u = s5ua.rearrange("p (m t) -> p m t", t=NT)
                yga, kyg = region(10, 0, 1024, BF16)
                yg = yga.rearrange("p (m t) -> p m t", t=NT)
                for m in range(4):
                    wa, kwa = load_w(ws5, m, 1024)
                    p, kp = getps()
                    for k in range(8):
                        mm(p, wa[:, k * 128:(k + 1) * 128], umT[:, k, :], kwa + kum, kp, start=(k == 0), stop=(k == 7))
                    cpy("act", s5u[:, m, :], p, kp, ks5u)
                if tt_i == 0:
                    S.op("pool", lambda e: e.memset(zst[:], 0.0), writes=[("zst", p_) for p_ in range(16)])
                def s5_front(pt):
                    m, q = pt // 4, pt % 4
                    par = pt % 2
                    cosT, kco = region(11, par * 256, 256, BF16)
                    sinT, ksi = region(12, par * 256, 256, BF16)
                    S.dma("sp", lambda e, cosT=cosT, pt=pt: e.dma_start(out=cosT, in_=tab_d[pt, 0, :, tg0:tg0 + 512]),
                          reads=[("tab", pt)], writes=kco)
                    S.dma("sp", lambda e, sinT=sinT, pt=pt: e.dma_start(out=sinT, in_=tab_d[pt, 1, :, tg0:tg0 + 512]),
                          reads=[("tab", pt)], writes=ksi)
                    bi_ = (2 * pt) % 6
                    pbr, kbr = psum[:, bi_ * 512:(bi_ + 1) * 512], [("ps", bi_)]
                    pbi, kbi = psum[:, (bi_ + 1) * 512:(bi_ + 2) * 512], [("ps", bi_ + 1)]
                    rows = slice(32 * q, 32 * q + 32)
                    mm(pbr, BbT[rows, m, 0, :], s5u[rows, m, :], ["BbT"] + ks5u, kbr, tp=(32 * q, 0))
                    mm(pbi, BbT[rows, m, 1, :], s5u[rows, m, :], ["BbT"] + ks5u, kbi, tp=(32 * q, 0))
                    return cosT, kco, sinT, ksi, pbr, kbr, pbi, kbi

                def s5_back(pt, fr):
                    cosT, kco, sinT, ksi, pbr, kbr, pbi, kbi = fr
                    m, q = pt // 4, pt % 4
                    par = pt % 2
                    rows = slice(32 * q, 32 * q + 32)
                    py, kpy = psum[:, (6 + m % 2) * 512:(7 + m % 2) * 512], [("ps", 6 + m % 2)]
                    sA, sB, sC, sD = ((13, 14, 15, 16) if par == 0 else (20, 21, 22, 23))
                    m1, km1 = region(sA, 0, 256, BF16)
                    m2, km2 = region(sA, 256, 256, BF16)
                    m3, km3 = region(sA, 512, 256, BF16)
                    m4, km4 = region(sA, 768, 256, BF16)
                    brb, kbrb = region(sB, 0, 256, BF16)
                    bib, kbib = region(sB, 256, 256, BF16)
                    cre2, kcr = region(sB, 512, 256, BF16)
                    cim2, kci = region(sB, 768, 256, BF16)
                    zre, kzr = region(sC, 0, 512)
                    zim, kzi = region(sC, 512, 512)
                    zrb, kzrb = region(sD, 0, 256, BF16)
                    zib, kzib = region(sD, 256, 256, BF16)
                    cpy("act", brb, pbr, kbr, kbrb)
                    cpy("act", bib, pbi, kbi, kbib)
                    tt("dve", m1, brb, cosT, ALU.mult, kbrb + kco, km1)
                    tt("dve", m2, bib, sinT, ALU.mult, kbib + ksi, km2)
                    tt("dve", m3, bib, cosT, ALU.mult, kbib + kco, km3)
                    tt("dve", m4, brb, sinT, ALU.mult, kbrb + ksi, km4)
                    tt("dve", cre2, m1, m2, ALU.add, km1 + km2, kcr)
                    tt("dve", cim2, m3, m4, ALU.subtract, km3 + km4, kci)
                    rb = rmag[:, pt:pt + 1].broadcast_to([128, 512])
                    S.op("dve", lambda e: e.tensor_tensor_scan(
                        out=zre, data0=rb, data1=cre2, initial=zst[:, pt, 0:1], op0=ALU.mult, op1=ALU.add),
                        reads=kcr + K5 + [("zst", pt)], writes=kzr)
                    cpy("act", zrb, zre, kzr, kzrb)
                    S.op("dve", lambda e: e.tensor_tensor_scan(
                        out=zim, data0=rb, data1=cim2, initial=zst[:, pt, 1:2], op0=ALU.mult, op1=ALU.add),
                        reads=kci + K5 + [("zst", pt)], writes=kzi)
                    cpy("act", zib, zim, kzi, kzib)
                    cpy("act", zst[:, pt, 0:1], zre[:, 511:512], kzr, [("zst", pt)])
                    cpy("act", zst[:, pt, 1:2], zim[:, 511:512], kzi, [("zst", pt)])
                    tt("dve", m1, zrb, cosT, ALU.mult, kzrb + kco, km1)
                    tt("dve", m3, zrb, sinT, ALU.mult, kzrb + ksi, km3)
                    tt("dve", m2, zib, sinT, ALU.mult, kzib + ksi, km2)
                    tt("dve", m4, zib, cosT, ALU.mult, kzib + kco, km4)
                    xre, kxr = region(sD, 512, 256, BF16)
                    xim, kxi = region(sD, 768, 256, BF16)
                    tt("dve", xre, m1, m2, ALU.subtract, km1 + km2, kxr)
                    tt("dve", xim, m3, m4, ALU.add, km3 + km4, kxi)
                    mm(py[rows, :], CT[:, pt, 0, :], xre, ["CT"] + kxr, kpy, start=True, stop=False, tp=(0, 32 * q))
                    mm(py[rows, :], CT[:, pt, 1, :], xim, ["CT"] + kxi, kpy, start=False, stop=False, tp=(0, 32 * q))
                    mm(py[rows, :], Dd[rows, m, :], s5u[rows, m, :], ["Dd"] + ks5u, kpy, start=False, stop=True,
                       tp=(32 * q, 32 * q))
                    if q == 3:
                        act(yg[:, m, :], py, AF.Gelu, kpy, kyg)

                fr_next = s5_front(0)
                for pt in range(16):
                    fr_cur = fr_next
                    if pt + 1 < 16:
                        fr_next = s5_front(pt + 1)
                    s5_back(pt, fr_cur)
                for m in range(4):
                    wa, kwa = load_w(wglu, m, 512)
                    p, kp = getps()
                    for k in range(4):
                        mm(p, wa[:, k * 128:(k + 1) * 128], yg[:, k, :], kwa + kyg, kp, start=(k == 0), stop=(k == 3))
                    sg, ksg = region(13 + (m % 2), 0, 512)
                    act(sg, p, AF.Sigmoid, kp + ["vecs"], ksg, bias=vecs[:, V_BGLU + m:V_BGLU + m + 1], scale=1.0)
                    tt("dve", ybh[:, m, :], yg[:, m, :], sg, ALU.mult, kyg + ksg, kyb)
            else:
                S.op("pool", lambda e: e.memset(ybha, 0.0), writes=kyb)

            mga, kmg = region(2, 0, 2048, BF16)
            mg = mga.rearrange("p (m t) -> p m t", t=NT)
            for m in range(8):
                wa, kwa = load_w(wpa, m, 512)
                wb_, kwb = load_w(wpb, m, 512)
                wc, kwc = load_w(wga, m, 1024)
                wd, kwd = load_w(wgb, m, 1024)
                pa, kpa = getps()
                pb_, kpb_ = getps()
                pga, kpga = getps()
                pgb, kpgb = getps()
                for k in range(4):
                    mm(pa, wa[:, k * 128:(k + 1) * 128], ogT[:, k, :], kwa + kog, kpa, start=(k == 0), stop=(k == 3))
                for k in range(4):
                    mm(pb_, wb_[:, k * 128:(k + 1) * 128], ybh[:, k, :], kwb + kyb, kpb_, start=(k == 0), stop=(k == 3))
                for k in range(8):
                    mm(pga, wc[:, k * 128:(k + 1) * 128], umT[:, k, :], kwc + kum, kpga, start=(k == 0), stop=(k == 7))
                for k in range(8):
                    mm(pgb, wd[:, k * 128:(k + 1) * 128], umT[:, k, :], kwd + kum, kpgb, start=(k == 0), stop=(k == 7))
                sa, ksa = region(9 + (m % 2), 0, 512)
                sb_, ksb = region(9 + (m % 2), 512, 512)
                act(sa, pga, AF.Sigmoid, kpga, ksa)
                act(sb_, pgb, AF.Sigmoid, kpgb, ksb)
                tt("dve", sa, sa, pa, ALU.mult, ksa + kpa, ksa)
                tt("dve", sb_, sb_, pb_, ALU.mult, ksb + kpb_, ksb)
                tt("dve", mg[:, m, :], sa, sb_, ALU.add, ksa + ksb, kmg)
            for m in range(8):
                wa, kwa = load_w(wout, m, 1024)
                p, kp = getps()
                for k in range(8):
                    mm(p, wa[:, k * 128:(k + 1) * 128], mg[:, k, :], kwa + kmg, kp, start=(k == 0), stop=(k == 7))
                hs = hT[:, m, t0:t0 + 512]
                stt("dve", hs, p, HG[:, 1, m, b:b + 1], hs, ALU.mult, ALU.add, kp + ["HG", ("h", m, t0 // 512)], [("h", m, t0 // 512)])

        def final_out(b, t0, n, g0):
            for s in range(n // 512):
                tsl = slice(t0 + s * 512, t0 + (s + 1) * 512)
                pss, kp = getps()
                for k in range(8):
                    sq, ksq = region(15, (k % 2) * 256, 256, BF16)
                    act(sq, hT[:, k, tsl], AF.Square, [("h", k, tsl.start // 512)], ksq)
                    mm(pss, onesD, sq, ksq + ["consb"], kp, start=(k == 0), stop=(k == 7))
                rstd, krs = region(16, 0, 512)
                act(rstd, pss, AF.Ln, kp + ["s5vc"], krs, bias=epsc, scale=1.0)
                act(rstd, rstd, AF.Exp, krs, krs, scale=-0.5)
                ob, kob = region(0 + 4 * (s % 2), 0, 4096)
                ob3 = ob.rearrange("p (k t) -> p k t", t=512)
                for k in range(8):
                    tm, ktm = region(17, (k % 2) * 512, 512)
                    tt("dve", tm, hT[:, k, tsl], rstd, ALU.mult, [("h", k, tsl.start // 512)] + krs, ktm)
                    act(ob3[:, k, :], tm, AF.Copy, ktm + ["vecs"], kob, scale=vecs[:, V_GF + k:V_GF + k + 1])
                gsl = slice(g0 + t0 + s * 512, g0 + t0 + (s + 1) * 512)
                S.dma("sp", lambda e, ob3=ob3, gsl=gsl, b=b: e.dma_start(out=oT[b, :, :, gsl], in_=ob3), reads=kob,
                      writes=["out"])

        for b in range(nseq):
            for hf in range(2):
                for k in range(8):
                    S.dma("sp", lambda e, b=b, k=k, hf=hf: e.dma_start(out=hT[:, k, :], in_=xT[b, :, k, hf * 1024:(hf + 1) * 1024]),
                          writes=[("h", k, 0), ("h", k, 1)])
                if stage >= 1:
                    ffn(b, 0, 0, 0)
                for _ in bg_tables:
                    pass
                if stage >= 2:
                    for t in range(2):
                        mixer(b, hf * 2 + t)
                if stage >= 4:
                    ffn(b, 0, 1, 2)
                final_out(b, 0, 1024, hf * 1024)
        S.final_wait("sp", ["out"])
        S.run(block)
    return nc, S.ninst


def tile_w(W, Mw):
    K, M = W.shape
    return np.ascontiguousarray(W.reshape(K // 128, 128, M // Mw, Mw).transpose(2, 1, 0, 3)).reshape(M // Mw, 128, (K // 128) * Mw)


def prep_shared(I):
    f = np.float32
    sh = {}
    sh["wada"] = tile_w(I["w_ada"][0], 128)
    sh["bada"] = np.ascontiguousarray(I["b_ada"][0].reshape(72, 128).T)
    for nm, a, bname in (("w1", "w1_ffn1", "w1_ffn2"), ("w3", "w3_ffn1", "w3_ffn2"), ("w2", "w2_ffn1", "w2_ffn2")):
        sh[nm + "a"] = tile_w(I[a][0], 128)
        sh[nm + "b"] = tile_w(I[bname][0], 128)
    Win = I["w_in"][0]
    sh["wq"] = tile_w(Win[:, 0:512], 64)
    sh["wk"] = tile_w(Win[:, 512:1024], 64)
    sh["wv"] = tile_w(Win[:, 1024:1536], 128)
    sh["wz"] = tile_w(Win[:, 1536:2048], 256)
    sh["wba"] = tile_w(Win[:, 2048:2064], 16)
    sh["ws5"] = tile_w(Win[:, 2064:2576], 128)
    sh["wga"] = tile_w(Win[:, 2576:3600], 128)
    sh["wgb"] = tile_w(Win[:, 3600:4624], 128)
    sh["wpa"] = tile_w(I["w_proj_a"][0], 128)
    sh["wpb"] = tile_w(I["w_proj_b"][0], 128)
    sh["wglu"] = tile_w(I["w_glu"][0], 128)
    sh["wout"] = tile_w(I["w_out"][0], 128)
    vecs = np.zeros((128, NV), f)

    def col8(v):
        return v.reshape(-1, 128).T
    vecs[:, V_G1:V_G1 + 8] = col8(I["g_ffn1"][0])
    vecs[:, V_GM:V_GM + 8] = col8(I["g_mix"][0])
    vecs[:, V_G2:V_G2 + 8] = col8(I["g_ffn2"][0])
    vecs[:, V_GF:V_GF + 8] = col8(I["g_final"])
    vecs[:, V_BGLU:V_BGLU + 4] = col8(I["b_glu"][0])
    vecs[:, V_DSK:V_DSK + 4] = col8(I["d_skip"][0])
    conv = I["conv_qkv"][0]
    cq = conv[:, 0:512].T.reshape(8, 64, 4).transpose(1, 0, 2).reshape(64, 32)
    ck = conv[:, 512:1024].T.reshape(8, 64, 4).transpose(1, 0, 2).reshape(64, 32)
    vecs[0:64, V_CQK:V_CQK + 32] = cq
    vecs[64:128, V_CQK:V_CQK + 32] = ck
    vecs[:, V_CV:V_CV + 16] = conv[:, 1024:1536].T.reshape(4, 128, 4).transpose(1, 0, 2).reshape(128, 16)

    def st(a):
        return a.reshape(16, 2, 64).transpose(1, 2, 0).reshape(128, 16)
    vecs[:, V_LRE:V_LRE + 16] = st(I["lam_re"][0])
    vecs[:, V_LIM:V_LIM + 16] = st(I["lam_im"][0])
    vecs[:, V_LST:V_LST + 16] = st(np.broadcast_to(I["log_step"][0][:, None], (32, 64)))
    vecs[:, V_ALOG:V_ALOG + 8] = np.broadcast_to(I["a_log"][0][None, :], (128, 8))
    vecs[:, V_DTB:V_DTB + 8] = np.broadcast_to(I["dt_bias"][0][None, :], (128, 8))
    vecs[:, V_GON:V_GON + 64] = np.broadcast_to(I["g_onorm"][0][None, :], (128, 64))
    sh["vecs"] = vecs
    cons = np.zeros((128, NCON), f)
    jj, ii = np.meshgrid(np.arange(128), np.arange(128), indexing="ij")
    cons[:, C_ID:C_ID + 128] = np.eye(128)
    cons[:, C_TRI:C_TRI + 128] = (jj <= ii)
    cons[:, C_NEG:C_NEG + 128] = np.where(jj <= ii, 0.0, -1e4)
    cons[:, C_STR:C_STR + 128] = (jj < ii)
    cons[:, C_ONE:C_ONE + 128] = 1.0
    cons[:, C_IP32:C_IP32 + 32] = (np.arange(128)[:, None] % 32 == np.arange(32)[None, :])
    sh["cons"] = cons
    mk = np.zeros((128, 9 * 128), f)
    mk[:, 0:128] = (jj // 8 == ii // 8)
    for li, bsz in enumerate((8, 16, 32, 64)):
        mb = (jj // (2 * bsz) == ii // (2 * bsz)) & (jj % (2 * bsz) < bsz) & (ii % (2 * bsz) >= bsz)
        mk[:, (1 + 2 * li) * 128:(2 + 2 * li) * 128] = mb
        mk[:, (2 + 2 * li) * 128:(3 + 2 * li) * 128] = mb.T
    sh["masks"] = mk

    def bl(a):
        return a.reshape(16, 2, 64, 16).transpose(1, 2, 0, 3).reshape(128, 256)

    def cl(a):
        return a.reshape(16, 2, 16, 64).transpose(1, 3, 0, 2).reshape(128, 256)
    sh["s5bc"] = np.concatenate([bl(I["b_re"][0]), bl(I["b_im"][0]), cl(I["c_re"][0]), cl(I["c_im"][0])], axis=1)
    return {k: np.ascontiguousarray(v, dtype=f) for k, v in sh.items()}


def prep_core(I, core, nseq=4):
    x = I["x"][core * 4:core * 4 + nseq]
    xT = np.ascontiguousarray(x.transpose(0, 2, 1).reshape(nseq, 8, 128, SEQ).transpose(0, 2, 1, 3))
    c = I["c"][core * 4:core * 4 + 4]
    cT = np.ascontiguousarray(c.T.reshape(8, 128, 4).transpose(1, 0, 2))
    return {"xT": xT.astype(np.float32), "cT": cT.astype(np.float32)}


_CACHE = {}


def kernel(**inputs):
    I = {k: np.asarray(v) for k, v in inputs.items()}
    if "nc" not in _CACHE:
        _CACHE["nc"] = build(4)[0]
    nc = _CACHE["nc"]
    sh = prep_shared(I)
    in_maps = []
    for c in range(NCORE):
        d = dict(sh)
        d.update(prep_core(I, c))
        in_maps.append(d)
    res = run_bass_kernel_spmd(nc, in_maps, core_ids=list(range(NCORE)))
    out = np.empty((32, SEQ, D), np.float32)
    for c in range(NCORE):
        o = res.results[c]["oT"]
        out[c * 4:(c + 1) * 4] = o.transpose(0, 3, 2, 1).reshape(4, SEQ, D)
    return out
```
